# Optimizing a Trainium2 kernel written in Bass

```python
import math
import jax, jax.numpy as jnp
from jax import lax
import numpy as np

D_MODEL = 1024
BATCH = 2
SEQ = 8192
DEPTH = 1
DEC_BATCH = 128
DEC_SEQ = 4
PAST_LEN = 2048
PAGE_SIZE = 128

MIX_WIDTH = D_MODEL
DIFF_WIDTH = MIX_WIDTH // 2
RET_WIDTH = MIX_WIDTH - DIFF_WIDTH
DIFF_V_DIM = 128
N_DIFF_HEADS = DIFF_WIDTH // DIFF_V_DIM
DIFF_QK_DIM = DIFF_V_DIM // 2
ROT_DIM = DIFF_QK_DIM // 4
ROPE_THETA = 500000.0
RET_V_DIM = 128
N_RET_HEADS = RET_WIDTH // RET_V_DIM
RET_QK_DIM = RET_V_DIM // 2
RET_THETA = 10000.0
RET_CHUNK = 128
Q_BLOCK = 128
D_FF = -(-8 * D_MODEL // (3 * 256)) * 256
NORM_EPS = 1e-6
SUBLN_EPS = 1e-5

SPLIT_SIZES = (
    N_DIFF_HEADS * 2 * DIFF_QK_DIM,
    N_DIFF_HEADS * 2 * DIFF_QK_DIM,
    N_DIFF_HEADS * DIFF_V_DIM,
    N_RET_HEADS * RET_QK_DIM,
    N_RET_HEADS * RET_QK_DIM,
    N_RET_HEADS * RET_V_DIM,
    RET_WIDTH,
)
IN_WIDTH = sum(SPLIT_SIZES)
SPLIT_POINTS = tuple(int(s) for s in np.cumsum(SPLIT_SIZES)[:-1])

kernel_name = 'hymba_diffattn_retnet_step'


def rms_norm(x, g, eps):
    xf = x.astype(jnp.float32)
    y = xf * lax.rsqrt(jnp.mean(xf * xf, axis=-1, keepdims=True) + eps)
    if g is not None:
        y = y * g.astype(jnp.float32)
    return y.astype(x.dtype)


def rope_table(pos, dim, theta):
    inv = 1.0 / (jnp.float32(theta) ** (jnp.arange(0, dim, 2, dtype=jnp.float32) / dim))
    ang = pos.astype(jnp.float32)[:, None] * inv[None, :]
    return jnp.cos(ang), jnp.sin(ang)


def rotate(x, cos, sin):
    xf = x.astype(jnp.float32)
    x1, x2 = jnp.split(xf, 2, axis=-1)
    return jnp.concatenate([x1 * cos - x2 * sin, x2 * cos + x1 * sin], axis=-1).astype(x.dtype)


def partial_rotary(x, cos, sin):
    return jnp.concatenate([rotate(x[..., :ROT_DIM], cos, sin), x[..., ROT_DIM:]], axis=-1)


def project_mixer_inputs(h, w_in_l, cos_d, sin_d, cos_r, sin_r):
    B, T, _ = h.shape
    z = h @ w_in_l
    dq, dk, dv, rq, rk, rv, rg = jnp.split(z, SPLIT_POINTS, axis=-1)
    dq = partial_rotary(dq.reshape(B, T, N_DIFF_HEADS, 2, DIFF_QK_DIM), cos_d, sin_d) * DIFF_QK_DIM ** -0.5
    dk = partial_rotary(dk.reshape(B, T, N_DIFF_HEADS, 2, DIFF_QK_DIM), cos_d, sin_d)
    dv = dv.reshape(B, T, N_DIFF_HEADS, DIFF_V_DIM)
    rq = rotate(rq.reshape(B, T, N_RET_HEADS, RET_QK_DIM), cos_r, sin_r)
    rk = rotate(rk.reshape(B, T, N_RET_HEADS, RET_QK_DIM), cos_r, sin_r) * RET_QK_DIM ** -0.5
    rv = rv.reshape(B, T, N_RET_HEADS, RET_V_DIM)
    return dq, dk, dv, rq, rk, rv, rg


def diff_attend(q, k, v, mask, lam):
    s = jnp.einsum('bqhmd,bkhmd->bhmqk', q, k).astype(jnp.float32)
    s = jnp.where(mask, s, -1e30)
    p = jax.nn.softmax(s, axis=-1)
    a = p[:, :, 0] - lam * p[:, :, 1]
    return jnp.einsum('bhqk,bkhe->bqhe', a.astype(v.dtype), v)


def prompt_diff_attention(q, k, v, lam):
    B, S = q.shape[:2]
    n_blk = S // Q_BLOCK
    qb = q.reshape(B, n_blk, Q_BLOCK, *q.shape[2:]).swapaxes(0, 1)
    kpos = jnp.arange(S)

    def one_block(args):
        q_blk, i = args
        qpos = i * Q_BLOCK + jnp.arange(Q_BLOCK)
        return diff_attend(q_blk, k, v, kpos[None, :] <= qpos[:, None], lam)

    out = lax.map(one_block, (qb, jnp.arange(n_blk)))
    return out.swapaxes(0, 1).reshape(B, S, *out.shape[3:])


def retention_chunk(q, k, v, state, log_gamma):
    C = q.shape[1]
    qf, kf, vf = (t.astype(jnp.float32) for t in (q, k, v))
    st = state.astype(jnp.float32)
    idx = jnp.arange(C, dtype=jnp.float32)
    rel = idx[:, None] - idx[None, :]
    decay = jnp.where(rel >= 0, jnp.exp(log_gamma[:, None, None] * jnp.maximum(rel, 0.0)), 0.0)
    scores = jnp.einsum('bihd,bjhd->bhij', qf, kf) * decay
    inner = jnp.einsum('bhij,bjhe->bihe', scores, vf)
    q_decay = jnp.exp(log_gamma[None, :] * (idx[:, None] + 1.0))
    cross = jnp.einsum('bihd,bhde->bihe', qf, st) * q_decay[None, :, :, None]
    k_decay = jnp.exp(log_gamma[None, :] * (C - 1.0 - idx[:, None]))
    new_state = (jnp.exp(log_gamma * C)[None, :, None, None] * st
                 + jnp.einsum('bjhd,bjhe->bhde', kf * k_decay[None, :, :, None], vf))
    return inner + cross, new_state


def prompt_retention(q, k, v, log_gamma):
    B, S, H, dk = q.shape
    dv = v.shape[-1]
    n_chunks = S // RET_CHUNK

    def chunks(t):
        return t.reshape(B, n_chunks, RET_CHUNK, *t.shape[2:]).swapaxes(0, 1)

    def step(state, qkv):
        out, state = retention_chunk(*qkv, state, log_gamma)
        return state, out

    state0 = jnp.zeros((B, H, dk, dv), jnp.float32)
    state, out = lax.scan(step, state0, (chunks(q), chunks(k), chunks(v)))
    return out.swapaxes(0, 1).reshape(B, S, H, dv), state


def merge_head_groups(diff_o, ret_o, ret_gate, subln_g, lam_init, w_out_l):
    B, T = diff_o.shape[:2]
    d = rms_norm(diff_o, subln_g, SUBLN_EPS) * (1.0 - lam_init)
    r = rms_norm(ret_o, None, NORM_EPS).reshape(B, T, RET_WIDTH).astype(ret_gate.dtype)
    r = jax.nn.silu(ret_gate) * r
    cat = jnp.concatenate([d.reshape(B, T, DIFF_WIDTH).astype(r.dtype), r], axis=-1)
    return cat @ w_out_l


def swiglu(h, w_in_l, w_out_l):
    g, u = jnp.split(h @ w_in_l, 2, axis=-1)
    return (jax.nn.silu(g) * u) @ w_out_l


def setup_inputs(seed: int = 0) -> dict:
    key = jax.random.key(seed)
    ks = jax.random.split(key, 16)
    n_pages = PAST_LEN // PAGE_SIZE
    n_used = DEC_BATCH * n_pages
    n_phys = n_used + max(1, n_used // 4)

    def normal(k, shape, scale):
        return jax.random.normal(k, shape, jnp.float32) * scale

    page_table = jax.random.permutation(ks[0], n_phys)[:n_used].reshape(DEC_BATCH, n_pages).astype(jnp.int32)
    return {
        'x_prompt': normal(ks[1], (BATCH, SEQ, D_MODEL), 1.0),
        'x_sample': normal(ks[2], (DEC_BATCH, DEC_SEQ, D_MODEL), 1.0),
        'cache_diff_k': normal(ks[3], (DEPTH, n_phys, PAGE_SIZE, N_DIFF_HEADS, 2, DIFF_QK_DIM), 1.0),
        'cache_diff_v': normal(ks[4], (DEPTH, n_phys, PAGE_SIZE, N_DIFF_HEADS, DIFF_V_DIM), 1.0),
        'state_ret': normal(ks[5], (DEPTH, DEC_BATCH, N_RET_HEADS, RET_QK_DIM, RET_V_DIM), 0.5),
        'page_table': page_table,
        'norm_mix_g': 1.0 + normal(ks[6], (DEPTH, D_MODEL), 0.02),
        'w_in': normal(ks[7], (DEPTH, D_MODEL, IN_WIDTH), D_MODEL ** -0.5),
        'diff_lambda': normal(ks[8], (DEPTH, 4, DIFF_QK_DIM), 0.1),
        'diff_subln_g': 1.0 + normal(ks[9], (DEPTH, DIFF_V_DIM), 0.02),
        'w_out': normal(ks[10], (DEPTH, MIX_WIDTH, D_MODEL), MIX_WIDTH ** -0.5),
        'norm_ffn_g': 1.0 + normal(ks[11], (DEPTH, D_MODEL), 0.02),
        'w_ffn_in': normal(ks[12], (DEPTH, D_MODEL, 2 * D_FF), D_MODEL ** -0.5),
        'w_ffn_out': normal(ks[13], (DEPTH, D_FF, D_MODEL), D_FF ** -0.5),
        'norm_final_g': 1.0 + normal(ks[14], (D_MODEL,), 0.02),
    }


def reference(x_prompt, x_sample, cache_diff_k, cache_diff_v, state_ret, page_table,
              norm_mix_g, w_in, diff_lambda, diff_subln_g, w_out, norm_ffn_g,
              w_ffn_in, w_ffn_out, norm_final_g):
    S = x_prompt.shape[1]
    DB, T = x_sample.shape[:2]
    past = page_table.shape[1] * cache_diff_k.shape[2]
    pos_p = jnp.arange(S)
    pos_s = past + jnp.arange(T)

    def tables(pos):
        cd, sd = rope_table(pos, ROT_DIM, ROPE_THETA)
        cr, sr = rope_table(pos, RET_QK_DIM, RET_THETA)
        return cd[:, None, None, :], sd[:, None, None, :], cr[:, None, :], sr[:, None, :]

    tab_p = tables(pos_p)
    tab_s = tables(pos_s)
    mask_s = jnp.arange(past + T)[None, :] <= pos_s[:, None]
    log_gamma = jnp.log(1.0 - 2.0 ** (-5.0 - jnp.arange(N_RET_HEADS, dtype=jnp.float32)))

    xp, xs = x_prompt, x_sample
    k_p, v_p, r_p, k_s, v_s, r_s = [], [], [], [], [], []
    for l in range(DEPTH):
        lam_init = 0.8 - 0.6 * math.exp(-0.3 * l)
        lp = diff_lambda[l].astype(jnp.float32)
        lam = jnp.exp(jnp.sum(lp[0] * lp[1])) - jnp.exp(jnp.sum(lp[2] * lp[3])) + lam_init

        hp = rms_norm(xp, norm_mix_g[l], NORM_EPS)
        dq, dk, dv, rq, rk, rv, rg = project_mixer_inputs(hp, w_in[l], *tab_p)
        diff_o = prompt_diff_attention(dq, dk, dv, lam)
        ret_o, ret_state_p = prompt_retention(rq, rk, rv, log_gamma)
        xp = xp + merge_head_groups(diff_o, ret_o, rg, diff_subln_g[l], lam_init, w_out[l])
        xp = xp + swiglu(rms_norm(xp, norm_ffn_g[l], NORM_EPS), w_ffn_in[l], w_ffn_out[l])
        k_p.append(dk)
        v_p.append(dv)
        r_p.append(ret_state_p.astype(xp.dtype))

        hs = rms_norm(xs, norm_mix_g[l], NORM_EPS)
        sq, sk, sv, tq, tk, tv, tg = project_mixer_inputs(hs, w_in[l], *tab_s)
        k_past = cache_diff_k[l][page_table].reshape(DB, past, N_DIFF_HEADS, 2, DIFF_QK_DIM)
        v_past = cache_diff_v[l][page_table].reshape(DB, past, N_DIFF_HEADS, DIFF_V_DIM)
        k_all = jnp.concatenate([k_past.astype(sk.dtype), sk], axis=1)
        v_all = jnp.concatenate([v_past.astype(sv.dtype), sv], axis=1)
        diff_os = diff_attend(sq, k_all, v_all, mask_s, lam)
        ret_os, ret_state_s = retention_chunk(tq, tk, tv, state_ret[l], log_gamma)
        xs = xs + merge_head_groups(diff_os, ret_os, tg, diff_subln_g[l], lam_init, w_out[l])
        xs = xs + swiglu(rms_norm(xs, norm_ffn_g[l], NORM_EPS), w_ffn_in[l], w_ffn_out[l])
        k_s.append(sk)
        v_s.append(sv)
        r_s.append(ret_state_s.astype(xs.dtype))

    y_prompt = rms_norm(xp, norm_final_g, NORM_EPS)
    y_sample = rms_norm(xs, norm_final_g, NORM_EPS)
    k_prompt = jnp.stack(k_p)
    v_prompt = jnp.stack(v_p)
    ret_prompt = jnp.stack(r_p)
    k_sample = jnp.stack(k_s)
    v_sample = jnp.stack(v_s)
    ret_sample = jnp.stack(r_s)
    return (y_prompt, y_sample, k_prompt, v_prompt, ret_prompt, k_sample, v_sample, ret_sample)
```

```python
import numpy as np
import ml_dtypes
from contextlib import ExitStack
import concourse.bass as bass
import concourse.mybir as mybir
from concourse.bass_utils import run_bass_kernel_spmd

F32, BF16, I32 = mybir.dt.float32, mybir.dt.bfloat16, mybir.dt.int32
ALU = mybir.AluOpType
AF = mybir.ActivationFunctionType
AX = mybir.AxisListType

import os
NPH = int(os.environ.get('KDBG_NPH', '2560'))
GROUPS = [[0, 1, 2, 3], [4, 5, 6, 7], [8, 9, 10, 11], [12, 13, 14, 15, 16]]


class Prog:
    ENG = ('pe', 'act', 'dve', 'pool', 'sp')

    def __init__(self):
        self.items = {e: [] for e in self.ENG}
        self.cnt = {e: 0 for e in self.ENG}
        self.waited = {e: {} for e in self.ENG}
        self.buf = {}
        self.dcnt = {}
        self.sem_names = list(self.ENG)

    def _need(self, eng, tok, lst):
        if tok is None:
            return
        s, v = tok
        if s == eng and eng == 'pe':
            return
        if self.waited[eng].get(s, 0) >= v:
            return
        self.waited[eng][s] = v
        lst.append((s, v))

    def op(self, eng, fn, r=(), w=(), sig=True, dma=None, waits=()):
        wl = []
        for k in r:
            st = self.buf.get(k)
            if st:
                self._need(eng, st[0], wl)
        for k in w:
            st = self.buf.get(k)
            if st:
                self._need(eng, st[0], wl)
                for t in st[1]:
                    self._need(eng, t, wl)
        for t in waits:
            self._need(eng, t, wl)
        for (s, v) in wl:
            self.items[eng].append(('w', s, v))
        if dma is not None:
            sn = 'd_' + dma
            if sn not in self.dcnt:
                self.dcnt[sn] = 0
                self.sem_names.append(sn)
            self.dcnt[sn] += 16
            tok = (sn, self.dcnt[sn])
            self.items[eng].append(('d', fn, sn))
        else:
            if sig:
                self.cnt[eng] += 1
                tok = (eng, self.cnt[eng])
            else:
                tok = (eng, self.cnt[eng] + 1)
            self.items[eng].append(('o', fn, sig))
        for k in r:
            self.buf.setdefault(k, [None, []])[1].append(tok)
        for k in w:
            self.buf[k] = [tok, []]
        return tok

    def barrier(self):
        for e in self.ENG:
            wl = []
            for o in self.ENG:
                if o != e and self.cnt[o] > 0:
                    self._need(e, (o, self.cnt[o]), wl)
            for sn, v in self.dcnt.items():
                self._need(e, (sn, v), wl)
            for (s, v) in wl:
                self.items[e].append(('w', s, v))

    def emit(self, eng, e, sems):
        for it in self.items[eng]:
            if it[0] == 'w':
                e.wait_ge(sems[it[1]], it[2])
            elif it[0] == 'd':
                it[1](e).then_inc(sems[it[2]], 16)
            else:
                ins = it[1](e)
                if it[2]:
                    ins.then_inc(sems[eng], 1)
        self.items[eng] = []


def build_program(stage):
    nc = bass.Bass("TRN2", target_bir_lowering=False)
    D = {}

    def din(name, shape, dt):
        D[name] = nc.dram_tensor(name, shape, dt, kind="ExternalInput").ap()

    def dout(name, shape, dt):
        D[name] = nc.dram_tensor(name, shape, dt, kind="ExternalOutput").ap()

    if stage == 'A':
        din('xp', [8192, 1024], F32)
        din('xs', [256, 1024], F32)
        din('w_in', [128, 8 * 768], F32)
        din('gmix', [1, 1024], F32)
        din('gsl', [1, 128], F32)
        din('lamp', [1, 256], F32)
        din('ck', [NPH * 128, 128], F32)
        din('cv', [NPH * 128, 128], F32)
        din('pt', [1, 1024], I32)
        din('st_ret', [64, 64, 128], F32)
        din('identb', [128, 128], BF16)
        din('identf', [128, 128], F32)
        din('maskT', [128, 128], BF16)
        din('cosd', [128, 66 * 8], F32)
        din('sind', [128, 66 * 8], F32)
        din('cosr', [128, 66 * 32], F32)
        din('sinr', [128, 66 * 32], F32)
        din('dsc', [128, 8], F32)
        din('iota', [128, 1], F32)
        din('smask', [128, 256], F32)
        din('rmask', [128, 128], F32)
        din('kmask', [128, 32 * 64], BF16)
        dout('kv', [8448, 256], F32)
        dout('rp', [64, 128], F32)
        dout('rs', [64, 64, 128], F32)
        dout('cat_p', [8192, 256], BF16)
        dout('cat_s', [256, 256], BF16)
        cat_p, cat_s = D['cat_p'], D['cat_s']
    else:
        din('xres', [2176, 1024], F32)
        din('w_out', [128, 8 * 1024], F32)
        din('wfi', [22, 128, 2048], F32)
        din('wfo', [22, 128, 1024], F32)
        din('gffn', [1, 1024], F32)
        din('gfin', [1, 1024], F32)
        din('identb', [128, 128], BF16)
        din('catin', [2176, 1024], BF16)
        dout('y', [2176, 1024], F32)

    P = Prog()
    op = P.op

    with ExitStack() as es0:
        BK = [es0.enter_context(nc.psum_tensor(f"bk{i}", [128, 512], F32)) for i in range(8)]
        sems = {}
        sem_pool = [es0.enter_context(nc.semaphore(f"sp{i}")) for i in range(96)]
        nc.all_engine_barrier()
        for sh in sem_pool:
            nc.gpsimd.sem_clear(sh)
        nc.all_engine_barrier()

        def run_block(body):
            body()
            P.barrier()
            for sn in P.sem_names:
                if sn not in sems:
                    sems[sn] = sem_pool[len(sems)]
            with nc.Block() as block:
                @block.tensor
                def _(e):
                    P.emit('pe', e, sems)

                @block.scalar
                def _(e):
                    P.emit('act', e, sems)

                @block.vector
                def _(e):
                    P.emit('dve', e, sems)

                @block.gpsimd
                def _(e):
                    P.emit('pool', e, sems)

                @block.sync
                def _(e):
                    P.emit('sp', e, sems)

        if stage == 'A':
            with ExitStack() as es, ExitStack() as es1:
                def sb(name, shape, dt, stack=None):
                    return (stack or es).enter_context(nc.sbuf_tensor('s_' + name, shape, dt))

                def sb1(name, shape, dt):
                    return sb(name, shape, dt, es1)

                identb = sb('identb', [128, 128], BF16)
                identf = sb('identf', [128, 128], F32)
                onesf = sb('onesf', [128, 128], F32)
                maskT = sb('maskT', [128, 128], BF16)
                cosd = sb('cosd', [128, 66 * 8], F32)
                sind = sb('sind', [128, 66 * 8], F32)
                cosr = sb('cosr', [128, 66 * 32], F32)
                sinr = sb('sinr', [128, 66 * 32], F32)
                dsc = sb('dsc', [128, 8], F32)
                iota = sb('iota', [128, 1], F32)
                gmix = sb('gmixb', [128, 1024], F32)
                gsl = sb('gslb', [128, 128], F32)
                lp = sb('lp', [128, 256], F32)
                lsm = sb('lsm', [128, 8], F32)
                mhalf = sb('mhalf', [128, 1], F32)
                w_in = sb('w_in', [128, 8 * 768], BF16)
                x_sb = [sb(f'x{i}', [128, 1024], F32) for i in range(2)]
                xb = [sb(f'xb{i}', [128, 1024], BF16) for i in range(2)]
                xT = [sb(f'xT{i}', [128, 1024], BF16) for i in range(2)]
                junk = sb('junk', [128, 1024], BF16)
                stat = sb('stat', [128, 16], F32)
                z_sb = [sb(f'z{i}', [128, 768], F32) for i in range(2)]
                rt = sb('rt', [128, 4 * 32], F32)
                rtr = sb('rtr', [128, 4 * 64], F32)
                zb_qk = [sb(f'zbqk{i}', [128, 256], BF16) for i in range(2)]
                zb_r = [sb(f'zbr{i}', [128, 256], BF16) for i in range(2)]
                eg = [sb(f'eg{i}', [128, 128], F32) for i in range(2)]
                sg = sb('sg', [128, 8 * 128], F32)
                ro_sb = sb('ro_sb', [128, 128], F32)
                o_sb = sb('o_sb', [128, 128], F32)
                otmp = sb('otmp', [128, 128], F32)
                ep = sb('ep', [128, 16], F32)
                cat_st = [sb(f'cat{i}', [128, 256], BF16) for i in range(8)]
                KT = sb1('KT', [128, 8192], BF16)
                Vext = sb1('Vext', [128, 64 * 130], BF16)
                QT = [sb1(f'QT{i}', [128, 512], BF16) for i in range(2)]
                RT = [sb1(f'RT{i}', [64, 256], BF16) for i in range(2)]
                PT = [sb1(f'PT{i}', [128, 512], BF16) for i in range(4)]
                AT = sb1('AT', [128, 128], BF16)
                wst = sb1('wst', [64, 128], F32)
                st_bf = sb1('st_bf', [64, 128], BF16)
                rp_sb = sb1('rp_sb', [64, 128], F32)
                QTs = KTs = RTs = Qblk = ptb = idxa = kv_sb = KTp = PTs = Rsum = PN = smask = rmask = kmask = On = dT = roT = st_sb = st_bfs = ATs = Kpad = None
                def alloc_sample():
                    nonlocal QTs, KTs, RTs, Qblk, ptb, idxa, kv_sb, KTp, PTs, Rsum, PN, smask, rmask, kmask, On, dT, roT, st_sb, st_bfs, ATs, Kpad
                    QTs = sb('QTs', [128, 256], BF16)
                    KTs = sb('KTs', [128, 256], BF16)
                    RTs = [sb(f'RTs{i}', [64, 256], BF16) for i in range(2)]
                    Qblk = sb('Qblk', [128, 512], BF16)
                    ptb = sb('ptb', [128, 1024], I32)
                    idxa = sb('idxa', [128, 1024], I32)
                    kv_sb = [sb(f'kvs{i}', [128, 16 * 256], F32) for i in range(2)]
                    KTp = [sb(f'KTp{i}', [128, 2048], BF16) for i in range(2)]
                    PTs = [sb(f'PTs{i}', [128, 128], F32) for i in range(2)]
                    Rsum = sb('Rsum', [128, 512], F32)
                    PN = [sb(f'PN{i}', [128, 256], F32) for i in range(2)]
                    smask = sb('smask', [128, 256], F32)
                    rmask = sb('rmask', [128, 128], F32)
                    kmask = sb('kmask', [128, 2048], BF16)
                    On = sb('On', [128, 512], F32)
                    dT = sb('dT', [128, 256], F32)
                    roT = sb('roT', [128, 256], F32)
                    st_sb = sb('st_sb', [64, 64 * 128], F32)
                    st_bfs = sb('st_bfs', [64, 64 * 128], BF16)
                    ATs = [sb(f'ATs{i}', [128, 128], BF16) for i in range(2)]
                    Kpad = [sb(f'Kpad{i}', [128, 2048], BF16) for i in range(2)]

                def stage_a(part):
                    def ld(dst, src, key, q='sp'):
                        op(q, lambda e, dst=dst, src=src: e.dma_start(out=dst, in_=src), w=[key], dma='c_' + str(key))
                    nlam = lsm[:, 5:6]
                    if part == 1:
                        ld(identb[:], D['identb'][:, :], 'identb')
                        ld(identf[:], D['identf'][:, :], 'identf')
                        ld(maskT[:], D['maskT'][:, :], 'maskT')
                        ld(cosd[:], D['cosd'][:, :], 'cosd')
                        ld(sind[:], D['sind'][:, :], 'sind')
                        ld(cosr[:], D['cosr'][:, :], 'cosr')
                        ld(sinr[:], D['sinr'][:, :], 'sinr')
                        ld(dsc[:], D['dsc'][:, :], 'dsc')
                        ld(iota[:], D['iota'][:, :], 'iota')
                        ld(gmix[:], D['gmix'][0:1, :].broadcast_to([128, 1024]), 'gmix')
                        ld(gsl[:], D['gsl'][0:1, :].broadcast_to([128, 128]), 'gsl')
                        ld(lp[:], D['lamp'][0:1, :].broadcast_to([128, 256]), 'lp')
                        for kc in range(8):
                            op('pool', lambda e, kc=kc: e.dma_start(out=w_in[:, kc * 768:(kc + 1) * 768],
                                                                   in_=D['w_in'][:, kc * 768:(kc + 1) * 768]),
                               w=[('win', kc)], dma=f'win{kc}')
                        op('dve', lambda e: e.memset(mhalf[:], -0.5), w=['mhalf'])
                        op('dve', lambda e: e.memset(onesf[:], 1.0), w=['onesf'])
                        op('dve', lambda e: e.memset(wst[:], 0.0), w=['wst'])
                        op('dve', lambda e: e.memset(st_bf[:], 0.0), w=['st_bf'])
                        op('pool', lambda e: e.memset(Vext[:].rearrange("p (t c) -> p t c", c=130)[:, :, 128:130], 1.0), w=['Vones'])
                        op('dve', lambda e: e.tensor_scalar(out=gsl[:], in0=gsl[:], scalar1=0.8, scalar2=None, op0=ALU.mult),
                           r=['gsl'], w=['gsl'])
                        op('dve', lambda e: e.tensor_tensor(out=junk[:, 0:64], in0=lp[:, 0:64], in1=lp[:, 64:128], op=ALU.mult), r=['lp'], w=['junk'])
                        op('dve', lambda e: e.tensor_reduce(out=lsm[:, 0:1], in_=junk[:, 0:64], axis=AX.X, op=ALU.add), r=['junk'], w=['lsm0'])
                        op('dve', lambda e: e.tensor_tensor(out=junk[:, 0:64], in0=lp[:, 128:192], in1=lp[:, 192:256], op=ALU.mult), r=['lp'], w=['junk'])
                        op('dve', lambda e: e.tensor_reduce(out=lsm[:, 1:2], in_=junk[:, 0:64], axis=AX.X, op=ALU.add), r=['junk'], w=['lsm1'])
                        op('act', lambda e: e.activation(out=lsm[:, 2:4], in_=lsm[:, 0:2], func=AF.Exp), r=['lsm0', 'lsm1'], w=['lsm2'])
                        op('dve', lambda e: e.tensor_tensor(out=lsm[:, 4:5], in0=lsm[:, 3:4], in1=lsm[:, 2:3], op=ALU.subtract),
                           r=['lsm2'], w=['lsm4'])
                        op('dve', lambda e: e.tensor_scalar(out=lsm[:, 5:6], in0=lsm[:, 4:5], scalar1=-0.2, scalar2=None, op0=ALU.add),
                           r=['lsm4'], w=['nlam'])

                    def rstd_chain(ssap, vap, outap, n, eps, keyin, keyout):
                        op('dve', lambda e: e.tensor_scalar(out=vap, in0=ssap, scalar1=1.0 / n, scalar2=eps,
                                                            op0=ALU.mult, op1=ALU.add), r=[keyin], w=[keyout + '_v'])
                        op('act', lambda e: e.activation(out=outap, in_=vap, func=AF.Ln), r=[keyout + '_v'], w=[keyout])
                        op('act', lambda e: e.activation(out=outap, in_=outap, func=AF.Exp, scale=-0.5), r=[keyout], w=[keyout])
                        for _ in range(1):
                            op('dve', lambda e: e.scalar_tensor_tensor(out=ssap, in0=outap, scalar=vap, in1=outap,
                                                                       op0=ALU.mult, op1=ALU.mult), r=[keyout, keyout + '_v'], w=[keyin])
                            op('dve', lambda e: e.tensor_scalar(out=ssap, in0=ssap, scalar1=-0.5, scalar2=1.5,
                                                                op0=ALU.mult, op1=ALU.add), r=[keyin], w=[keyin])
                            op('dve', lambda e: e.tensor_tensor(out=outap, in0=outap, in1=ssap, op=ALU.mult),
                               r=[keyin, keyout], w=[keyout])

                    def proj_tile(t):
                        sl = t % 2
                        samp = t >= 64
                        ts = t - 64
                        src = D['xs'][ts * 128:(ts + 1) * 128, :] if samp else D['xp'][t * 128:(t + 1) * 128, :]
                        op('sp', lambda e: e.dma_start(out=x_sb[sl][:], in_=src), w=[('x', sl)], dma=f'x{sl}')
                        op('dve', lambda e: e.tensor_tensor(out=junk[:], in0=x_sb[sl][:], in1=x_sb[sl][:], op=ALU.mult), r=[('x', sl)], w=['junk'])
                        op('dve', lambda e: e.tensor_reduce(out=stat[:, sl:sl + 1], in_=junk[:], axis=AX.X, op=ALU.add), r=['junk'], w=[('ss', sl)])
                        rstd_chain(stat[:, sl:sl + 1], stat[:, 2 + sl:3 + sl], stat[:, 4 + sl:5 + sl], 1024.0, 1e-6,
                                   ('ss', sl), f'rstd{sl}')
                        op('dve', lambda e: e.scalar_tensor_tensor(out=xb[sl][:], in0=x_sb[sl][:], scalar=stat[:, 4 + sl:5 + sl],
                                                                   in1=gmix[:], op0=ALU.mult, op1=ALU.mult),
                           r=[('x', sl), f'rstd{sl}', 'gmix'], w=[('xb', sl)])
                        for half in range(2):
                            for k in range(4):
                                kc = half * 4 + k
                                op('pe', lambda e, k=k, kc=kc: e.matmul(BK[0][:, k * 128:(k + 1) * 128],
                                                                       lhsT=xb[sl][:, kc * 128:(kc + 1) * 128], rhs=identb[:],
                                                                       start=True, stop=True),
                                   r=[('xb', sl), 'identb'], w=['bk0'] if k == 0 else [], sig=(k == 3))
                            if half == 0:
                                op('act', lambda e: e.activation(out=xT[sl][:, 0:512], in_=BK[0][:, :], func=AF.Copy),
                                   r=['bk0'], w=[('xT', sl, 0)])
                            else:
                                op('dve', lambda e: e.tensor_copy(out=xT[sl][:, 512:1024], in_=BK[0][:, :]),
                                   r=['bk0'], w=[('xT', sl, 1)])
                        for kc in range(8):
                            op('pe', lambda e, kc=kc: e.matmul(BK[1][:, :], lhsT=xT[sl][:, kc * 128:(kc + 1) * 128],
                                                               rhs=w_in[:, kc * 768:kc * 768 + 512], start=(kc == 0), stop=(kc == 7)),
                               r=[('xT', sl, 0), ('xT', sl, 1), ('win', kc)], w=['bk1'] if kc == 0 else [], sig=False)
                            op('pe', lambda e, kc=kc: e.matmul(BK[2][:, 0:256], lhsT=xT[sl][:, kc * 128:(kc + 1) * 128],
                                                               rhs=w_in[:, kc * 768 + 512:kc * 768 + 768], start=(kc == 0), stop=(kc == 7)),
                               r=[], w=['bk2z'] if kc == 0 else [], sig=(kc == 7))
                        zs = z_sb[sl]
                        op('act', lambda e: e.activation(out=zs[:, 0:512], in_=BK[1][:, :], func=AF.Copy),
                           r=['bk1'], w=[('zqk', sl), ('zv', sl), ('zr', sl)])
                        op('dve', lambda e: e.tensor_copy(out=zs[:, 512:768], in_=BK[2][:, 0:256]),
                           r=['bk2z'], w=[('zb', sl)])
                        zq = zs[:, 0:256].rearrange("p (g d) -> p g d", d=64)
                        x1, x2 = zq[:, :, 0:8], zq[:, :, 8:16]
                        cd = cosd[:, t * 8:(t + 1) * 8].unsqueeze(1).broadcast_to([128, 4, 8])
                        sd = sind[:, t * 8:(t + 1) * 8].unsqueeze(1).broadcast_to([128, 4, 8])
                        rtv = rt[:].rearrange("p (a g d) -> p a g d", a=4, g=4)
                        for i, (a, b) in enumerate([(x1, cd), (x2, sd), (x2, cd), (x1, sd)]):
                            op('dve', lambda e, i=i, a=a, b=b: e.tensor_tensor(out=rtv[:, i], in0=a, in1=b, op=ALU.mult),
                               r=[('zqk', sl), 'cosd', 'sind'], w=[('rt', i)])
                        op('dve', lambda e: e.tensor_tensor(out=x1, in0=rtv[:, 0], in1=rtv[:, 1], op=ALU.subtract),
                           r=[('rt', 0), ('rt', 1)], w=[('zqk', sl)])
                        op('dve', lambda e: e.tensor_tensor(out=x2, in0=rtv[:, 2], in1=rtv[:, 3], op=ALU.add),
                           r=[('rt', 2), ('rt', 3)], w=[('zqk', sl)])
                        c0 = 2 if samp else 0
                        for qi in range(2):
                            col = 384 + qi * 64
                            op('dve', lambda e, col=col, qi=qi: e.tensor_scalar(out=zs[:, col:col + 64], in0=zs[:, col:col + 64],
                                                                                 scalar1=dsc[:, c0 + qi:c0 + qi + 1], scalar2=None,
                                                                                 op0=ALU.mult),
                               r=[('zr', sl), 'dsc'], w=[('zr', sl)])
                        zr = zs[:, 384:512].rearrange("p (g d) -> p g d", d=64)
                        y1, y2 = zr[:, :, 0:32], zr[:, :, 32:64]
                        cr = cosr[:, t * 32:(t + 1) * 32].unsqueeze(1).broadcast_to([128, 2, 32])
                        sr = sinr[:, t * 32:(t + 1) * 32].unsqueeze(1).broadcast_to([128, 2, 32])
                        rrv = rtr[:].rearrange("p (a g d) -> p a g d", a=4, g=2)
                        for i, (a, b) in enumerate([(y1, cr), (y2, sr), (y2, cr), (y1, sr)]):
                            op('dve', lambda e, i=i, a=a, b=b: e.tensor_tensor(out=rrv[:, i], in0=a, in1=b, op=ALU.mult),
                               r=[('zr', sl), 'cosr', 'sinr'], w=[('rtr', i)])
                        op('dve', lambda e: e.tensor_tensor(out=y1, in0=rrv[:, 0], in1=rrv[:, 1], op=ALU.subtract),
                           r=[('rtr', 0), ('rtr', 1)], w=[('zr', sl)])
                        op('dve', lambda e: e.tensor_tensor(out=y2, in0=rrv[:, 2], in1=rrv[:, 3], op=ALU.add),
                           r=[('rtr', 2), ('rtr', 3)], w=[('zr', sl)])
                        kvtok = op('sp', lambda e: e.dma_start(out=D['kv'][t * 128:(t + 1) * 128, :], in_=zs[:, 128:384]),
                                   r=[('zqk', sl), ('zv', sl)], dma=f'kvo{sl}')
                        op('act', lambda e: e.activation(out=zb_qk[sl][:], in_=zs[:, 0:256], func=AF.Copy),
                           r=[('zqk', sl)], w=[('zbqk', sl)])
                        if not samp:
                            vx = Vext[:, t * 130:t * 130 + 128]
                            op('act', lambda e: e.activation(out=vx, in_=zs[:, 256:384], func=AF.Copy), r=[('zv', sl)], w=[('V', t)])
                        op('act', lambda e: e.activation(out=zb_r[sl][:], in_=zs[:, 384:640], func=AF.Copy),
                           r=[('zr', sl), ('zb', sl)], w=[('zbr', sl)])
                        s8 = t % 8
                        op('act', lambda e: e.activation(out=eg[sl][:], in_=zs[:, 640:768], func=AF.Exp, scale=-1.0),
                           r=[('zb', sl)], w=[('eg', sl)])
                        op('dve', lambda e: e.tensor_scalar(out=eg[sl][:], in0=eg[sl][:], scalar1=1.0, scalar2=None, op0=ALU.add),
                           r=[('eg', sl)], w=[('eg', sl)])
                        op('dve', lambda e: e.reciprocal(out=eg[sl][:], in_=eg[sl][:]), r=[('eg', sl)], w=[('eg', sl)])
                        op('dve', lambda e: e.tensor_tensor(out=sg[:, s8 * 128:(s8 + 1) * 128], in0=eg[sl][:], in1=zs[:, 640:768],
                                                             op=ALU.mult), r=[('eg', sl), ('zb', sl)], w=[('sg', s8)])
                        op('pe', lambda e: e.matmul(BK[0][:, 0:128], lhsT=zb_qk[sl][:, 0:128], rhs=identb[:], start=True, stop=True),
                           r=[('zbqk', sl)], w=['bk0'], sig=False)
                        op('pe', lambda e: e.matmul(BK[0][:, 128:256], lhsT=zb_qk[sl][:, 128:256], rhs=identb[:], start=True, stop=True),
                           sig=False)
                        op('pe', lambda e: e.matmul(BK[0][0:64, 256:384], lhsT=zb_r[sl][:, 0:64], rhs=identb[:], start=True, stop=True),
                           r=[('zbr', sl)], sig=False)
                        op('pe', lambda e: e.matmul(BK[0][0:64, 384:512], lhsT=zb_r[sl][:, 64:128], rhs=identb[:], start=True, stop=True),
                           sig=True)
                        if samp:
                            qdst, qk = QTs[:, ts * 128:(ts + 1) * 128], ('QTs', ts)
                            kdst, kk = KTs[:, ts * 128:(ts + 1) * 128], ('KTs', ts)
                            rdst, rk = RTs[ts][:], ('RTs', ts)
                        else:
                            I, qs = t // 4, t % 4
                            qdst, qk = QT[I % 2][:, qs * 128:(qs + 1) * 128], ('QT', I % 2, qs)
                            kdst, kk = KT[:, t * 128:(t + 1) * 128], ('KT', t)
                            rdst, rk = RT[sl][:], ('RT', sl)
                        op('act', lambda e: e.activation(out=qdst, in_=BK[0][:, 0:128], func=AF.Copy), r=['bk0'], w=[qk])
                        op('dve', lambda e: e.tensor_copy(out=kdst, in_=BK[0][:, 128:256]), r=['bk0'], w=[kk])
                        op('dve', lambda e: e.tensor_copy(out=rdst, in_=BK[0][0:64, 256:512]), r=['bk0'], w=[rk])
                        return kvtok

                    def ret_epilogue(src_ps, srckey, s8, cslot):
                        op('act', lambda e: e.activation(out=ro_sb[:], in_=src_ps, func=AF.Copy), r=[srckey], w=['ro_sb'])
                        op('dve', lambda e: e.tensor_tensor(out=junk[:, 0:128], in0=ro_sb[:], in1=ro_sb[:], op=ALU.mult), r=['ro_sb'], w=['junk'])
                        op('dve', lambda e: e.tensor_reduce(out=ep[:, 8:9], in_=junk[:, 0:128], axis=AX.X, op=ALU.add), r=['junk'], w=['ep8'])
                        rstd_chain(ep[:, 8:9], ep[:, 9:10], ep[:, 10:11], 128.0, 1e-6, 'ep8', 'rrstd')
                        op('dve', lambda e: e.scalar_tensor_tensor(out=cat_st[cslot][:, 128:256], in0=ro_sb[:], scalar=ep[:, 10:11],
                                                                   in1=sg[:, s8 * 128:(s8 + 1) * 128], op0=ALU.mult, op1=ALU.mult),
                           r=['ro_sb', 'rrstd', ('sg', s8)], w=[('catr', cslot)])

                    def diff_epilogue(cslot):
                        op('dve', lambda e: e.tensor_tensor(out=junk[:, 0:128], in0=o_sb[:], in1=o_sb[:], op=ALU.mult), r=['o_sb'], w=['junk'])
                        op('dve', lambda e: e.tensor_reduce(out=ep[:, 3:4], in_=junk[:, 0:128], axis=AX.X, op=ALU.add), r=['junk'], w=['ep3'])
                        rstd_chain(ep[:, 3:4], ep[:, 4:5], ep[:, 5:6], 128.0, 1e-5, 'ep3', 'orstd')
                        op('dve', lambda e: e.scalar_tensor_tensor(out=cat_st[cslot][:, 0:128], in0=o_sb[:], scalar=ep[:, 5:6],
                                                                   in1=gsl[:], op0=ALU.mult, op1=ALU.mult),
                           r=['o_sb', 'orstd', 'gsl'], w=[('catd', cslot)])

                    def retention(t):
                        sl = t % 2
                        op('pe', lambda e: e.matmul(BK[2][:, 256:384], lhsT=RT[sl][0:64, 128:256], rhs=RT[sl][0:64, 0:128],
                                                    start=True, stop=True), r=[('RT', sl)], w=['bk2s'])
                        op('dve', lambda e: e.tensor_tensor(out=AT[:], in0=BK[2][:, 256:384], in1=maskT[:], op=ALU.mult),
                           r=['bk2s', 'maskT'], w=['AT'])
                        op('pe', lambda e: e.matmul(BK[2][:, 384:512], lhsT=AT[:], rhs=zb_r[sl][:, 128:256], start=True, stop=False),
                           r=['AT', ('zbr', sl)], w=['bk2o'], sig=False)
                        op('pe', lambda e: e.matmul(BK[2][:, 384:512], lhsT=RT[sl][0:64, 0:128], rhs=st_bf[:], start=False, stop=True),
                           r=['st_bf', ('RT', sl)], sig=True)
                        op('pe', lambda e: e.matmul(BK[2][0:64, 256:384], lhsT=zb_r[sl][:, 64:128], rhs=zb_r[sl][:, 128:256],
                                                    start=True, stop=True), r=[('zbr', sl)], w=['bk2s'])
                        op('dve', lambda e: e.scalar_tensor_tensor(out=wst[:], in0=wst[:], scalar=dsc[0:64, 4:5],
                                                                   in1=BK[2][0:64, 256:384], op0=ALU.mult, op1=ALU.add),
                           r=['bk2s', 'dsc'], w=['wst'])
                        op('act', lambda e: e.activation(out=st_bf[:], in_=wst[:], func=AF.Copy, scale=dsc[0:64, 4:5]),
                           r=['wst', 'dsc'], w=['st_bf'])
                        ret_epilogue(BK[2][:, 384:512], 'bk2o', t % 8, t % 8)

                    def attention(I):
                        started = set()
                        qt = QT[I % 2]
                        nkb = 4 * I + 4
                        pti = [0]
                        for kb in range(nkb):
                            r = kb - 4 * I
                            q0 = max(r, 0) * 128
                            for m in range(2):
                                sbk = BK[3 + m]
                                op('pe', lambda e, m=m, kb=kb, q0=q0, sbk=sbk: e.matmul(
                                    sbk[:, q0:512], lhsT=KT[m * 64:(m + 1) * 64, kb * 128:(kb + 1) * 128],
                                    rhs=qt[m * 64:(m + 1) * 64, q0:512], start=True, stop=True),
                                   r=[('KT', kb)] + [('QT', I % 2, q) for q in range(4)], w=[('S', m)])
                                ps = pti[0] % 4
                                pti[0] += 1
                                ptt = PT[ps]
                                op('act', lambda e, q0=q0, sbk=sbk, ptt=ptt: e.activation(out=ptt[:, q0:512], in_=sbk[:, q0:512],
                                                                                          func=AF.Exp, scale=0.125),
                                   r=[('S', m)], w=[('PT', ps)])
                                if r >= 0:
                                    op('dve', lambda e, r=r, ptt=ptt: e.tensor_tensor(out=ptt[:, r * 128:(r + 1) * 128],
                                                                                       in0=ptt[:, r * 128:(r + 1) * 128],
                                                                                       in1=maskT[:], op=ALU.mult),
                                       r=[('PT', ps), 'maskT'], w=[('PT', ps)])
                                for qs in range(max(r, 0), 4):
                                    a = m * 4 + qs
                                    bank = 5 + a // 3
                                    c0 = (a % 3) * 129
                                    st = bank not in started
                                    started.add(bank)
                                    last = (kb == 4 * I + qs)
                                    op('pe', lambda e, bank=bank, c0=c0, qs=qs, kb=kb, st=st, last=last, ptt=ptt: e.matmul(
                                        BK[bank][:, c0:c0 + 129], lhsT=ptt[:, qs * 128:(qs + 1) * 128],
                                        rhs=Vext[:, kb * 130:kb * 130 + 129], start=st, stop=last, skip_group_check=True),
                                       r=[('PT', ps), ('V', kb), 'Vones'], w=[('acc', a)], sig=last)
                        for qs in range(4):
                            t = 4 * I + qs
                            a0, a1 = qs, 4 + qs
                            A0 = BK[5 + a0 // 3][:, (a0 % 3) * 129:(a0 % 3) * 129 + 129]
                            A1 = BK[5 + a1 // 3][:, (a1 % 3) * 129:(a1 % 3) * 129 + 129]
                            op('dve', lambda e, A0=A0: e.reciprocal(out=ep[:, 0:1], in_=A0[:, 128:129]), r=[('acc', a0)], w=['ep0'])
                            op('dve', lambda e, A1=A1: e.reciprocal(out=ep[:, 1:2], in_=A1[:, 128:129]), r=[('acc', a1)], w=['ep1'])
                            op('dve', lambda e: e.tensor_tensor(out=ep[:, 2:3], in0=ep[:, 1:2], in1=nlam, op=ALU.mult),
                               r=['ep1', 'nlam'], w=['ep2'])
                            op('dve', lambda e, A1=A1: e.tensor_scalar(out=otmp[:], in0=A1[:, 0:128], scalar1=ep[:, 2:3], scalar2=None,
                                                                       op0=ALU.mult), r=['ep2', ('acc', a1)], w=['otmp'])
                            op('dve', lambda e, A0=A0: e.scalar_tensor_tensor(out=o_sb[:], in0=A0[:, 0:128], scalar=ep[:, 0:1],
                                                                              in1=otmp[:], op0=ALU.mult, op1=ALU.add),
                               r=['ep0', 'otmp', ('acc', a0)], w=['o_sb'])
                            diff_epilogue(t % 8)

                    if part == 1:
                        cat_toks = []
                        for I in range(16):
                            for qs in range(4):
                                t = 4 * I + qs
                                proj_tile(t)
                                retention(t)
                            attention(I)
                            for qs in range(4):
                                t = 4 * I + qs
                                cs = t % 8
                                cat_toks.append(op('sp', lambda e, t=t, cs=cs: e.dma_start(out=cat_p[t * 128:(t + 1) * 128, :],
                                                                                           in_=cat_st[cs][:]),
                                                   r=[('catd', cs), ('catr', cs)], dma=f'cat{cs}'))
                        op('dve', lambda e: e.tensor_scalar(out=rp_sb[:], in0=wst[:], scalar1=dsc[0:64, 4:5], scalar2=None, op0=ALU.mult),
                           r=['wst', 'dsc'], w=['rp_sb'])
                        op('sp', lambda e: e.dma_start(out=D['rp'][:, :], in_=rp_sb[:]), r=['rp_sb'], dma='rp')

                        return
                    ld(smask[:], D['smask'][:, :], 'smask')
                    ld(rmask[:], D['rmask'][:, :], 'rmask')
                    ld(kmask[:], D['kmask'][:, :], 'kmask')
                    ld(ptb[:], D['pt'][0:1, :].broadcast_to([128, 1024]), 'ptb')
                    ld(st_sb[:].rearrange("d (b e) -> d b e", e=128), D['st_ret'].rearrange("b d e -> d b e"), 'st_sb')
                    op('pool', lambda e: e.memset(Qblk[:], 0.0), w=['Qblk'])
                    op('dve', lambda e: e.tensor_scalar(out=idxa[:], in0=ptb[:], scalar1=128.0, scalar2=iota[:, 0:1],
                                                        op0=ALU.mult, op1=ALU.add), r=['ptb', 'iota'], w=['idxa'])
                    proj_tile(64)
                    proj_tile(65)
                    for m in range(2):
                        qv = Qblk[m * 64:(m + 1) * 64, :].rearrange("p (b x) -> p b x", x=8)[:, :, m * 4:(m + 1) * 4]
                        sv = QTs[m * 64:(m + 1) * 64, :].rearrange("p (b q) -> p b q", q=4)
                        op('dve', lambda e, qv=qv, sv=sv: e.tensor_copy(out=qv, in_=sv), r=[('QTs', 0), ('QTs', 1), 'Qblk'],
                           w=[('Qb', m)])
                    OT = BK[7]
                    for kt in range(2):
                        op('pe', lambda e, kt=kt: e.matmul(BK[1][:, kt * 256:(kt + 1) * 256], lhsT=KTs[:, kt * 128:(kt + 1) * 128],
                                                           rhs=Qblk[:, kt * 256:(kt + 1) * 256], start=True, stop=True),
                           r=[('KTs', kt), ('Qb', 0), ('Qb', 1)], w=[('SN', kt)])
                        op('act', lambda e, kt=kt: e.activation(out=PN[kt][:], in_=BK[1][:, kt * 256:(kt + 1) * 256], func=AF.Exp,
                                                                scale=0.125), r=[('SN', kt)], w=[('PN', kt)])
                        op('dve', lambda e, kt=kt: e.tensor_tensor(out=PN[kt][:], in0=PN[kt][:], in1=smask[:], op=ALU.mult),
                           r=[('PN', kt), 'smask'], w=[('PN', kt)])
                        op('pe', lambda e, kt=kt: e.matmul(OT[:, kt * 256:(kt + 1) * 256], lhsT=z_sb[kt][:, 256:384], rhs=PN[kt][:],
                                                           start=(kt == 0), stop=False, skip_group_check=True),
                           r=[('PN', kt), ('zv', kt)], w=['OT'] if kt == 0 else [], sig=False)
                        op('pe', lambda e, kt=kt: e.matmul(BK[0][:, kt * 256:(kt + 1) * 256], lhsT=onesf[:], rhs=PN[kt][:],
                                                           start=True, stop=True), r=['onesf'], w=[('RN', kt)])
                    for b in range(64):
                        bs = b % 2
                        s4 = b % 4
                        for pg in range(16):
                            for kvi, nm in enumerate(('ck', 'cv')):
                                ktok = op('pool', lambda e, pg=pg, b=b, bs=bs, kvi=kvi, nm=nm: e.indirect_dma_start(
                                    out=kv_sb[bs][:, pg * 256 + kvi * 128:pg * 256 + (kvi + 1) * 128], out_offset=None,
                                    in_=D[nm][:, :],
                                    in_offset=bass.IndirectOffsetOnAxis(ap=idxa[:, b * 16 + pg:b * 16 + pg + 1], axis=0)),
                                   r=['idxa'], w=[('kvs', bs)] if (pg == 0 and kvi == 0) else [], dma=f'kvs{bs}')
                        P.buf[('kvs', bs)][0] = ktok
                        for g4 in range(4):
                            tb = BK[3 + g4 % 2]
                            for k in range(4):
                                pg = g4 * 4 + k
                                op('pe', lambda e, k=k, pg=pg, tb=tb, bs=bs: e.transpose(tb[:, k * 128:(k + 1) * 128],
                                                                                         kv_sb[bs][:, pg * 256:pg * 256 + 128],
                                                                                         identf[:]),
                                   r=[('kvs', bs), 'identf'], w=[('TB', g4 % 2)] if k == 0 else [], sig=(k == 3))
                            if g4 % 2 == 0:
                                op('act', lambda e, g4=g4, tb=tb, bs=bs: e.activation(out=KTp[bs][:, g4 * 512:(g4 + 1) * 512],
                                                                                      in_=tb[:, :], func=AF.Copy),
                                   r=[('TB', g4 % 2)], w=[('KTp', bs, g4)])
                            else:
                                op('dve', lambda e, g4=g4, tb=tb, bs=bs: e.tensor_copy(out=KTp[bs][:, g4 * 512:(g4 + 1) * 512],
                                                                                       in_=tb[:, :]),
                                   r=[('TB', g4 % 2)], w=[('KTp', bs, g4)])
                        for pg in range(16):
                            op('pe', lambda e, pg=pg, b=b, bs=bs, s4=s4: e.matmul(
                                BK[5][:, s4 * 128 + pg * 8:s4 * 128 + pg * 8 + 8], lhsT=KTp[bs][:, pg * 128:(pg + 1) * 128],
                                rhs=Qblk[:, b * 8:(b + 1) * 8], start=True, stop=True),
                               r=[('KTp', bs, pg // 4), ('Qb', 0), ('Qb', 1)], w=[('SB', s4)] if pg == 0 else [], sig=(pg == 15))
                        op('act', lambda e, bs=bs, s4=s4: e.activation(out=PTs[bs][:], in_=BK[5][:, s4 * 128:(s4 + 1) * 128],
                                                                       func=AF.Exp, scale=0.125), r=[('SB', s4)], w=[('PTs', bs)])
                        op('pe', lambda e, bs=bs, s4=s4: e.matmul(BK[6][:, s4 * 128:(s4 + 1) * 128], lhsT=onesf[:], rhs=PTs[bs][:],
                                                                  start=True, stop=True), r=[('PTs', bs), 'onesf'], w=[('RSB', s4)])
                        op('dve', lambda e, b=b, s4=s4: e.tensor_reduce(
                            out=Rsum[:, b * 8:(b + 1) * 8],
                            in_=BK[6][:, s4 * 128:(s4 + 1) * 128].rearrange("p (g c) -> p c g", c=8), axis=AX.X, op=ALU.add),
                           r=[('RSB', s4)], w=[('Rsum', b)])
                        for pg in range(16):
                            op('pe', lambda e, pg=pg, b=b, bs=bs: e.matmul(
                                OT[:, b * 8:(b + 1) * 8], lhsT=kv_sb[bs][:, pg * 256 + 128:pg * 256 + 256],
                                rhs=PTs[bs][:, pg * 8:(pg + 1) * 8], start=False, stop=(b == 63 and pg == 15), skip_group_check=True),
                               r=[('PTs', bs), ('kvs', bs)], w=['OT'] if (b == 63 and pg == 15) else [], sig=(pg == 15))
                    op('dve', lambda e: e.tensor_tensor(out=Rsum[:], in0=Rsum[:], in1=BK[0][:, :], op=ALU.add),
                       r=[('Rsum', b) for b in range(64)] + [('RN', 0), ('RN', 1)], w=['Rtot'])
                    op('dve', lambda e: e.reciprocal(out=Rsum[:], in_=Rsum[:]), r=['Rtot'], w=['Rtot'])
                    op('dve', lambda e: e.tensor_tensor(out=On[:], in0=OT[:, :], in1=Rsum[:], op=ALU.mult), r=['Rtot', 'OT'], w=['On'])
                    Onv = On[:].rearrange("p (b m q) -> p b m q", m=2, q=4)
                    op('dve', lambda e: e.scalar_tensor_tensor(out=dT[:].rearrange("p (b q) -> p b q", q=4), in0=Onv[:, :, 1, :],
                                                               scalar=nlam, in1=Onv[:, :, 0, :], op0=ALU.mult, op1=ALU.add),
                       r=['On', 'nlam'], w=['dT'])
                    for tt in range(2):
                        op('pe', lambda e, tt=tt: e.transpose(BK[3][:, tt * 128:(tt + 1) * 128], dT[:, tt * 128:(tt + 1) * 128],
                                                              identf[:]), r=['dT', 'identf'], w=[('dTt', tt)])
                    for tt in range(2):
                        op('act', lambda e, tt=tt: e.activation(out=o_sb[:], in_=BK[3][:, tt * 128:(tt + 1) * 128], func=AF.Copy),
                           r=[('dTt', tt)], w=['o_sb'])
                        diff_epilogue(tt)
                    op('act', lambda e: e.activation(out=st_bfs[:], in_=st_sb[:], func=AF.Copy), r=['st_sb'], w=['st_bfs'])
                    op('dve', lambda e: e.tensor_scalar(out=st_sb[:], in0=st_sb[:], scalar1=dsc[0:64, 5:6], scalar2=None, op0=ALU.mult),
                       r=['st_sb', 'dsc'], w=['st_sb'])
                    for kt in range(2):
                        op('pe', lambda e, kt=kt: e.matmul(BK[2][:, 256 + kt * 128:384 + kt * 128], lhsT=RTs[kt][0:64, 128:256],
                                                           rhs=RTs[kt][0:64, 0:128], start=True, stop=True),
                           r=[('RTs', kt)], w=[('SRs', kt)])
                        op('dve', lambda e, kt=kt: e.tensor_tensor(out=ATs[kt][:], in0=BK[2][:, 256 + kt * 128:384 + kt * 128],
                                                                   in1=rmask[:], op=ALU.mult), r=[('SRs', kt), 'rmask'], w=[('ATs', kt)])
                        op('dve', lambda e, kt=kt: e.tensor_tensor(
                            out=Kpad[kt][:].rearrange("p (b d) -> p b d", d=64),
                            in0=zb_r[kt][:, 64:128].unsqueeze(1).broadcast_to([128, 32, 64]),
                            in1=kmask[:].rearrange("p (b d) -> p b d", d=64), op=ALU.mult),
                           r=[('zbr', kt), 'kmask'], w=[('Kpad', kt)])
                    for kt in range(2):
                        op('pe', lambda e, kt=kt: e.matmul(BK[4][:, kt * 128:(kt + 1) * 128], lhsT=zb_r[kt][:, 128:256], rhs=ATs[kt][:],
                                                           start=(kt == 0), stop=False, skip_group_check=True),
                           r=[('ATs', kt), ('zbr', kt)], w=['roT'] if kt == 0 else [], sig=False)
                    for b in range(64):
                        op('pe', lambda e, b=b: e.matmul(BK[4][:, b * 4:(b + 1) * 4], lhsT=st_bfs[:, b * 128:(b + 1) * 128],
                                                         rhs=RTs[b // 32][0:64, (b % 32) * 4:(b % 32) * 4 + 4],
                                                         start=False, stop=(b == 63), skip_group_check=True),
                           r=['st_bfs'], w=['roT'] if b == 63 else [], sig=(b == 63))
                    op('act', lambda e: e.activation(out=roT[:], in_=BK[4][:, 0:256], func=AF.Copy), r=['roT'], w=['roT_sb'])
                    for tt in range(2):
                        op('pe', lambda e, tt=tt: e.transpose(BK[3][:, 256 + tt * 128:384 + tt * 128], roT[:, tt * 128:(tt + 1) * 128],
                                                              identf[:]), r=['roT_sb'], w=[('roTt', tt)])
                    for tt in range(2):
                        ret_epilogue(BK[3][:, 256 + tt * 128:384 + tt * 128], ('roTt', tt), (64 + tt) % 8, tt)
                    scat = []
                    for tt in range(2):
                        scat.append(op('sp', lambda e, tt=tt: e.dma_start(out=cat_s[tt * 128:(tt + 1) * 128, :], in_=cat_st[tt][:]),
                                       r=[('catd', tt), ('catr', tt)], dma=f'cat{tt}'))
                    for b in range(64):
                        bank = BK[5 + (b // 4) % 2]
                        op('pe', lambda e, b=b, bank=bank: e.matmul(bank[0:64, (b % 4) * 128:(b % 4 + 1) * 128],
                                                                    lhsT=Kpad[b // 32][:, (b % 32) * 64:(b % 32 + 1) * 64],
                                                                    rhs=zb_r[b // 32][:, 128:256], start=True, stop=True),
                           r=[('Kpad', b // 32)], w=[('DS', (b // 4) % 2)] if b % 4 == 0 else [], sig=(b % 4 == 3))
                        if b % 4 == 3:
                            b0 = b - 3
                            op('dve', lambda e, b0=b0, bank=bank: e.scalar_tensor_tensor(
                                out=st_sb[:, b0 * 128:(b0 + 4) * 128], in0=bank[0:64, :], scalar=dsc[0:64, 5:6],
                                in1=st_sb[:, b0 * 128:(b0 + 4) * 128], op0=ALU.mult, op1=ALU.add),
                               r=[('DS', (b // 4) % 2), 'st_sb', 'dsc'], w=[('stn', b0)])
                    op('sp', lambda e: e.dma_start(out=D['rs'].rearrange("b d e -> d b e"),
                                                   in_=st_sb[:].rearrange("d (b e) -> d b e", e=128)),
                       r=[('stn', b0) for b0 in range(0, 64, 4)], dma='rs')

                run_block(lambda: stage_a(1))
                es1.close()
                alloc_sample()
                run_block(lambda: stage_a(2))

        if stage == 'B':
            with ExitStack() as es:
                def sb(name, shape, dt):
                    return es.enter_context(nc.sbuf_tensor('s_' + name, shape, dt))

                identb = sb('identb2', [128, 128], BF16)
                mhalf = sb('mhalf2', [128, 1], F32)
                gffn = sb('gffn', [128, 1024], F32)
                gfin = sb('gfin', [128, 1024], F32)
                idx2 = sb('idx2', [128, 68], I32)
                w_out = sb('w_out', [128, 8 * 1024], BF16)
                wfo = sb('wfo', [128, 22 * 1024], BF16)
                wfi = [sb(f'wfi{i}', [128, 2048], BF16) for i in range(3)]
                wst32 = [sb(f'wst32_{i}', [128, 2048], F32) for i in range(2)]
                cat_sb = [sb(f'catsb{i}', [128, 1024], BF16) for i in range(2)]
                catT = [sb(f'catT{i}', [128, 1024], BF16) for i in range(2)]
                xres = [sb(f'xres{i}', [128, 1024], F32) for i in range(2)]
                xp1 = sb('xp1', [128, 5 * 1024], F32)
                junk = sb('junk2', [128, 1024], BF16)
                stat = sb('stat2', [128, 16], F32)
                h2 = [sb(f'h2{i}', [128, 1024], BF16) for i in range(2)]
                h2T = sb('h2T', [128, 8 * 640], BF16)
                actT = sb('actT', [128, 22 * 640], BF16)
                sgt = [sb(f'sgt{i}', [128, 512], F32) for i in range(2)]
                y_sb = [sb(f'y{i}', [128, 1024], F32) for i in range(2)]

                def stage_b():
                    def ld(dst, src, key, q='sp'):
                        op(q, lambda e, dst=dst, src=src: e.dma_start(out=dst, in_=src), w=[key], dma='c2_' + str(key))
                    ld(identb[:], D['identb'][:, :], 'identb2')
                    ld(gffn[:], D['gffn'][0:1, :].broadcast_to([128, 1024]), 'gffn')
                    ld(gfin[:], D['gfin'][0:1, :].broadcast_to([128, 1024]), 'gfin')
                    op('dve', lambda e: e.memset(mhalf[:], -0.5), w=['mhalf2'])
                    lcnt = [0]

                    def load_cast(dst, src, ncols, key):
                        sl_ = lcnt[0] % 2
                        lcnt[0] += 1
                        op('sp', lambda e: e.dma_start(out=wst32[sl_][:, 0:ncols], in_=src), w=[('wst32', sl_)], dma=f'wst{sl_}')
                        op('pool', lambda e: e.tensor_copy(out=dst, in_=wst32[sl_][:, 0:ncols]), r=[('wst32', sl_)], w=[key])
                    for kc in range(8):
                        load_cast(w_out[:, kc * 1024:(kc + 1) * 1024], D['w_out'][:, kc * 1024:(kc + 1) * 1024], 1024, ('wout', kc))
                    for j in range(22):
                        load_cast(wfo[:, j * 1024:(j + 1) * 1024], D['wfo'][j, :, :], 1024, ('wfo', j))

                    def rstd_chain(ssap, vap, outap, n, eps, keyin, keyout):
                        op('dve', lambda e: e.tensor_scalar(out=vap, in0=ssap, scalar1=1.0 / n, scalar2=eps,
                                                            op0=ALU.mult, op1=ALU.add), r=[keyin], w=[keyout + '_v'])
                        op('act', lambda e: e.activation(out=outap, in_=vap, func=AF.Sqrt), r=[keyout + '_v'], w=[keyout])
                        op('dve', lambda e: e.reciprocal(out=outap, in_=outap), r=[keyout], w=[keyout])
                        for _ in range(1):
                            op('dve', lambda e: e.scalar_tensor_tensor(out=ssap, in0=outap, scalar=vap, in1=outap,
                                                                       op0=ALU.mult, op1=ALU.mult), r=[keyout, keyout + '_v'], w=[keyin])
                            op('dve', lambda e: e.tensor_scalar(out=ssap, in0=ssap, scalar1=-0.5, scalar2=1.5,
                                                                op0=ALU.mult, op1=ALU.add), r=[keyin], w=[keyin])
                            op('dve', lambda e: e.tensor_tensor(out=outap, in0=outap, in1=ssap, op=ALU.mult),
                               r=[keyin, keyout], w=[keyout])

                    wcnt = [0]
                    for grp in GROUPS:
                        ng = len(grp)
                        NG = ng * 128
                        nch = [(0, min(512, NG))] + ([(512, NG - 512)] if NG > 512 else [])
                        for li, tt in enumerate(grp):
                            sl = tt % 2
                            op('sp', lambda e, tt=tt, sl=sl: e.dma_start(out=cat_sb[sl][:], in_=D['catin'][tt * 128:(tt + 1) * 128, :]),
                               w=[('catsb', sl)], dma=f'catsb{sl}')
                            op('sp', lambda e, tt=tt, sl=sl: e.dma_start(out=xres[sl][:], in_=D['xres'][tt * 128:(tt + 1) * 128, :]),
                               w=[('xres', sl)], dma=f'xres{sl}')
                            for half in range(2):
                                for k in range(4):
                                    ci = half * 4 + k
                                    op('pe', lambda e, k=k, ci=ci, sl=sl: e.matmul(BK[0][:, k * 128:(k + 1) * 128],
                                                                                   lhsT=cat_sb[sl][:, ci * 128:(ci + 1) * 128],
                                                                                   rhs=identb[:], start=True, stop=True),
                                       r=[('catsb', sl), 'identb2'], w=['bk0'] if k == 0 else [], sig=(k == 3))
                                if half == 0:
                                    op('act', lambda e, sl=sl: e.activation(out=catT[sl][:, 0:512], in_=BK[0][:, :], func=AF.Copy),
                                       r=['bk0'], w=[('catT', sl, 0)])
                                else:
                                    op('dve', lambda e, sl=sl: e.tensor_copy(out=catT[sl][:, 512:1024], in_=BK[0][:, :]),
                                       r=['bk0'], w=[('catT', sl, 1)])
                            for nh in range(2):
                                for ci in range(8):
                                    op('pe', lambda e, ci=ci, nh=nh, sl=sl: e.matmul(
                                        BK[1 + nh][:, :], lhsT=catT[sl][:, ci * 128:(ci + 1) * 128],
                                        rhs=w_out[:, ci * 1024 + nh * 512:ci * 1024 + (nh + 1) * 512], start=(ci == 0), stop=(ci == 7)),
                                       r=[('catT', sl, 0), ('catT', sl, 1), ('wout', ci)], w=[('bkx', nh)] if ci == 0 else [],
                                       sig=(ci == 7))
                            xv = xp1[:, li * 1024:(li + 1) * 1024]
                            for nh in range(2):
                                op('dve', lambda e, nh=nh, sl=sl, xv=xv: e.tensor_tensor(out=xv[:, nh * 512:(nh + 1) * 512],
                                                                                        in0=BK[1 + nh][:, :],
                                                                                        in1=xres[sl][:, nh * 512:(nh + 1) * 512],
                                                                                        op=ALU.add),
                                   r=[('bkx', nh), ('xres', sl)], w=[('xp1', li, nh)])
                            op('dve', lambda e, sl=sl, xv=xv: e.tensor_tensor(out=junk[:], in0=xv, in1=xv, op=ALU.mult), r=[('xp1', li, 0), ('xp1', li, 1)], w=['junk2'])
                            op('dve', lambda e, sl=sl, xv=xv: e.tensor_reduce(out=stat[:, sl:sl + 1], in_=junk[:], axis=AX.X, op=ALU.add), r=['junk2'], w=[('ss2', sl)])
                            rstd_chain(stat[:, sl:sl + 1], stat[:, 2 + sl:3 + sl], stat[:, 4 + sl:5 + sl], 1024.0, 1e-6,
                                       ('ss2', sl), f'rstd2{sl}')
                            op('dve', lambda e, sl=sl, xv=xv: e.scalar_tensor_tensor(out=h2[sl][:], in0=xv, scalar=stat[:, 4 + sl:5 + sl],
                                                                                     in1=gffn[:], op0=ALU.mult, op1=ALU.mult),
                               r=[('xp1', li, 0), ('xp1', li, 1), f'rstd2{sl}', 'gffn'], w=[('h2', sl)])
                            h2v = h2T[:].rearrange("p (k n) -> p k n", n=640)
                            for half in range(2):
                                for k in range(4):
                                    kc = half * 4 + k
                                    op('pe', lambda e, k=k, kc=kc, sl=sl: e.matmul(BK[0][:, k * 128:(k + 1) * 128],
                                                                                   lhsT=h2[sl][:, kc * 128:(kc + 1) * 128],
                                                                                   rhs=identb[:], start=True, stop=True),
                                       r=[('h2', sl)], w=['bk0'] if k == 0 else [], sig=(k == 3))
                                dst = h2v[:, half * 4:(half + 1) * 4, li * 128:(li + 1) * 128]
                                srcv = BK[0][:, :].rearrange("p (k n) -> p k n", n=128)
                                if half == 0:
                                    op('act', lambda e, dst=dst, srcv=srcv: e.activation(out=dst, in_=srcv, func=AF.Copy),
                                       r=['bk0'], w=[('h2T', li, 0)])
                                else:
                                    op('dve', lambda e, dst=dst, srcv=srcv: e.tensor_copy(out=dst, in_=srcv),
                                       r=['bk0'], w=[('h2T', li, 1)])
                        h2keys = [('h2T', li, hf) for li in range(ng) for hf in range(2)]
                        av = actT[:].rearrange("p (j n) -> p j n", n=640)
                        pc = 0
                        for j in range(22):
                            ws = wcnt[0] % 3
                            wcnt[0] += 1
                            load_cast(wfi[ws][:], D['wfi'][j, :, :], 2048, ('wfi', ws))
                            for (n0, nn) in nch:
                                pb = pc % 2
                                pc += 1
                                G, U = BK[3 + 2 * pb], BK[4 + 2 * pb]
                                for kc in range(8):
                                    op('pe', lambda e, kc=kc, ws=ws, n0=n0, nn=nn, G=G: e.matmul(
                                        G[:, 0:nn], lhsT=wfi[ws][:, kc * 256:kc * 256 + 128], rhs=h2v[:, kc, n0:n0 + nn],
                                        start=(kc == 0), stop=(kc == 7)),
                                       r=[('wfi', ws)] + h2keys, w=[('G', pb)] if kc == 0 else [], sig=(kc == 7))
                                for kc in range(8):
                                    op('pe', lambda e, kc=kc, ws=ws, n0=n0, nn=nn, U=U: e.matmul(
                                        U[:, 0:nn], lhsT=wfi[ws][:, kc * 256 + 128:kc * 256 + 256], rhs=h2v[:, kc, n0:n0 + nn],
                                        start=(kc == 0), stop=(kc == 7)),
                                       r=[('wfi', ws)], w=[('U', pb)] if kc == 0 else [], sig=(kc == 7))
                                op('act', lambda e, pb=pb, nn=nn, G=G: e.activation(out=sgt[pb][:, 0:nn], in_=G[:, 0:nn], func=AF.Silu),
                                   r=[('G', pb)], w=[('sgt', pb)])
                                op('dve', lambda e, pb=pb, nn=nn, n0=n0, j=j, U=U: e.tensor_tensor(
                                    out=av[:, j, n0:n0 + nn], in0=sgt[pb][:, 0:nn], in1=U[:, 0:nn], op=ALU.mult),
                                   r=[('sgt', pb), ('U', pb)], w=[('actT', j, n0)])
                        akeys = [('actT', j, n0) for j in range(22) for (n0, nn) in nch]
                        for li, tt in enumerate(grp):
                            ysl = tt % 2
                            xv = xp1[:, li * 1024:(li + 1) * 1024]
                            for nh in range(2):
                                for j in range(22):
                                    op('pe', lambda e, j=j, nh=nh, li=li: e.matmul(
                                        BK[1 + nh][:, :], lhsT=av[:, j, li * 128:(li + 1) * 128],
                                        rhs=wfo[:, j * 1024 + nh * 512:j * 1024 + (nh + 1) * 512], start=(j == 0), stop=(j == 21)),
                                       r=(akeys if j == 0 else []) + [('wfo', j)], w=[('bkx', nh)] if j == 0 else [], sig=(j == 21))
                                op('dve', lambda e, nh=nh, xv=xv: e.tensor_tensor(out=xv[:, nh * 512:(nh + 1) * 512],
                                                                                 in0=BK[1 + nh][:, :],
                                                                                 in1=xv[:, nh * 512:(nh + 1) * 512], op=ALU.add),
                                   r=[('bkx', nh)], w=[('xp1', li, nh)])
                            op('dve', lambda e, ysl=ysl, xv=xv: e.tensor_tensor(out=junk[:], in0=xv, in1=xv, op=ALU.mult), r=[('xp1', li, 0), ('xp1', li, 1)], w=['junk2'])
                            op('dve', lambda e, ysl=ysl, xv=xv: e.tensor_reduce(out=stat[:, 8 + ysl:9 + ysl], in_=junk[:], axis=AX.X, op=ALU.add), r=['junk2'], w=[('ss3', ysl)])
                            rstd_chain(stat[:, 8 + ysl:9 + ysl], stat[:, 10 + ysl:11 + ysl], stat[:, 12 + ysl:13 + ysl], 1024.0, 1e-6,
                                       ('ss3', ysl), f'rstd3{ysl}')
                            op('dve', lambda e, ysl=ysl, xv=xv: e.scalar_tensor_tensor(out=y_sb[ysl][:], in0=xv,
                                                                                       scalar=stat[:, 12 + ysl:13 + ysl], in1=gfin[:],
                                                                                       op0=ALU.mult, op1=ALU.mult),
                               r=[('xp1', li, 0), ('xp1', li, 1), f'rstd3{ysl}', 'gfin'], w=[('y', ysl)])
                            op('sp', lambda e, ysl=ysl, tt=tt: e.dma_start(out=D['y'][tt * 128:(tt + 1) * 128, :], in_=y_sb[ysl][:]),
                               r=[('y', ysl)], dma=f'yo{ysl}')

                run_block(stage_b)
        nc.all_engine_barrier()
        for sh in sem_pool:
            nc.gpsimd.sem_clear(sh)
        nc.all_engine_barrier()
    return nc


_CACHE = {}


def _consts(h):
    key = ('c', h)
    if key in _CACHE:
        return _CACHE[key]
    p = np.arange(128)
    pos = np.zeros((128, 66), np.float32)
    for t in range(64):
        pos[:, t] = t * 128 + p
    for t in (64, 65):
        pos[:, t] = 2048 + (p % 4)

    def table(dim, theta):
        inv = (1.0 / (np.float32(theta) ** (np.arange(0, dim, 2, dtype=np.float32) / np.float32(dim)))).astype(np.float32)
        ang = (pos[:, :, None] * inv[None, None, :]).astype(np.float32)
        return np.cos(ang).astype(np.float32), np.sin(ang).astype(np.float32)
    cd, sd = table(16, 500000.0)
    cr, sr = table(64, 10000.0)
    gamma = 1.0 - 2.0 ** (-5.0 - h)
    dsc = np.zeros((128, 8), np.float32)
    dsc[:, 0] = gamma ** (p + 1.0)
    dsc[:, 1] = gamma ** (-(p + 1.0)) / 8.0
    dsc[:, 2] = gamma ** ((p % 4) + 1.0)
    dsc[:, 3] = gamma ** (-((p % 4) + 1.0)) / 8.0
    dsc[:, 4] = gamma ** 128.0
    dsc[:, 5] = gamma ** 4.0
    maskT = (p[:, None] <= p[None, :]).astype(np.float32)
    col = np.arange(256)
    smask = ((p[:, None] // 4 == col[None, :] // 8) & (p[:, None] % 4 <= col[None, :] % 4)).astype(np.float32)
    rmask = ((p[:, None] // 4 == p[None, :] // 4) & (p[:, None] % 4 <= p[None, :] % 4)).astype(np.float32)
    kb = np.arange(32)
    kmask = np.repeat((p[:, None] // 4 == kb[None, :]).astype(np.float32), 64, axis=1)
    out = dict(
        identb=np.eye(128, dtype=np.float32).astype(ml_dtypes.bfloat16),
        identf=np.eye(128, dtype=np.float32),
        maskT=maskT.astype(ml_dtypes.bfloat16),
        cosd=np.ascontiguousarray(cd.reshape(128, 66 * 8)), sind=np.ascontiguousarray(sd.reshape(128, 66 * 8)),
        cosr=np.ascontiguousarray(cr.reshape(128, 66 * 32)), sinr=np.ascontiguousarray(sr.reshape(128, 66 * 32)),
        dsc=dsc, iota=p.astype(np.float32).reshape(128, 1), smask=smask, rmask=rmask,
        kmask=kmask.astype(ml_dtypes.bfloat16),
    )
    _CACHE[key] = out
    return out


def kernel(x_prompt, x_sample, cache_diff_k, cache_diff_v, state_ret, page_table,
           norm_mix_g, w_in, diff_lambda, diff_subln_g, w_out, norm_ffn_g,
           w_ffn_in, w_ffn_out, norm_final_g):
    f = lambda a: np.ascontiguousarray(np.asarray(a, dtype=np.float32))
    x_prompt, x_sample = f(x_prompt), f(x_sample)
    w_in0, w_out0, wfi0, wfo0 = f(w_in)[0], f(w_out)[0], f(w_ffn_in)[0], f(w_ffn_out)[0]
    ck, cv = np.asarray(cache_diff_k)[0][:NPH], np.asarray(cache_diff_v)[0][:NPH]
    st = f(state_ret)[0]
    pt = np.asarray(page_table).astype(np.int32)
    if NPH != 2560:
        pt = pt % NPH
    for stg in ('A', 'B'):
        if ('nc', stg) not in _CACHE:
            _CACHE[('nc', stg)] = build_program(stg)
    rows = np.concatenate([np.arange(hp * 128, hp * 128 + 128) if half == 0 else np.arange(512 + hp * 128, 512 + hp * 128 + 128)
                           for hp in range(4) for half in range(2)])
    w_out_l = np.ascontiguousarray(w_out0[rows].reshape(8, 128, 1024).transpose(1, 0, 2).reshape(128, 8192))
    wg = wfi0[:, :2816].reshape(8, 128, 22, 128)
    wu = wfi0[:, 2816:].reshape(8, 128, 22, 128)
    wfi_l = np.ascontiguousarray(np.concatenate([wg, wu], axis=3).transpose(2, 1, 0, 3).reshape(22, 128, 2048))
    wfo_l = np.ascontiguousarray(wfo0.reshape(22, 128, 1024))
    xs_flat = x_sample.reshape(512, 1024)
    p = np.arange(128)
    in_maps = []
    in_maps_b = []
    for c in range(8):
        g, h = c // 4, c % 4
        cols = np.concatenate([np.arange(h * 128, h * 128 + 128), 512 + np.arange(h * 128, h * 128 + 128),
                               1024 + np.arange(h * 128, h * 128 + 128), 1536 + np.arange(h * 64, h * 64 + 64),
                               1792 + np.arange(h * 64, h * 64 + 64), 2048 + np.arange(h * 128, h * 128 + 128),
                               2560 + np.arange(h * 128, h * 128 + 128)])
        w_in_l = np.ascontiguousarray(w_in0[:, cols].reshape(8, 128, 768).transpose(1, 0, 2).reshape(128, 8 * 768))
        srow = np.concatenate([(np.arange(8 * c, 8 * c + 8)[:, None] * 4 + np.arange(4)[None, :]).reshape(-1),
                               ((64 + np.arange(8 * c, 8 * c + 8))[:, None] * 4 + np.arange(4)[None, :]).reshape(-1)])
        xres = np.concatenate([x_prompt[0, 1024 * c:1024 * (c + 1)], x_prompt[1, 1024 * c:1024 * (c + 1)],
                               xs_flat[srow], xs_flat[srow]], axis=0)
        cst = _consts(h)
        mA = dict(
            xp=x_prompt[g], xs=np.ascontiguousarray(xs_flat[256 * g:256 * (g + 1)]), w_in=w_in_l,
            gmix=f(norm_mix_g).reshape(1, 1024), gsl=f(diff_subln_g).reshape(1, 128), lamp=f(diff_lambda).reshape(1, 256),
            ck=np.ascontiguousarray(ck[:, :, h].reshape(NPH * 128, 128), dtype=np.float32),
            cv=np.ascontiguousarray(cv[:, :, h].reshape(NPH * 128, 128), dtype=np.float32),
            pt=np.ascontiguousarray(pt[64 * g:64 * (g + 1)].reshape(1, 1024)),
            st_ret=np.ascontiguousarray(st[64 * g:64 * (g + 1), h]),
        )
        mA.update(cst)
        mB = dict(xres=np.ascontiguousarray(xres), w_out=w_out_l, wfi=wfi_l, wfo=wfo_l,
                  gffn=f(norm_ffn_g).reshape(1, 1024), gfin=f(norm_final_g).reshape(1, 1024), identb=cst['identb'])
        in_maps.append(mA)
        in_maps_b.append(mB)
    resA = run_bass_kernel_spmd(_CACHE[('nc', 'A')], in_maps, core_ids=list(range(8)))
    RA = resA.results
    for c in range(8):
        catin = np.zeros((2176, 1024), ml_dtypes.bfloat16)
        for hp in range(4):
            catin[0:1024, hp * 256:(hp + 1) * 256] = np.asarray(RA[hp]['cat_p'])[1024 * c:1024 * (c + 1)]
            catin[1024:2048, hp * 256:(hp + 1) * 256] = np.asarray(RA[4 + hp]['cat_p'])[1024 * c:1024 * (c + 1)]
            catin[2048:2080, hp * 256:(hp + 1) * 256] = np.asarray(RA[hp]['cat_s'])[32 * c:32 * (c + 1)]
            catin[2080:2112, hp * 256:(hp + 1) * 256] = np.asarray(RA[4 + hp]['cat_s'])[32 * c:32 * (c + 1)]
        catin[2112:2176] = catin[2048:2112]
        in_maps_b[c]['catin'] = catin
    resB = run_bass_kernel_spmd(_CACHE[('nc', 'B')], in_maps_b, core_ids=list(range(8)))
    RB = resB.results
    y_prompt = np.zeros((2, 8192, 1024), np.float32)
    y_sample = np.zeros((512, 1024), np.float32)
    k_prompt = np.zeros((1, 2, 8192, 4, 2, 64), np.float32)
    v_prompt = np.zeros((1, 2, 8192, 4, 128), np.float32)
    ret_prompt = np.zeros((1, 2, 4, 64, 128), np.float32)
    k_sample = np.zeros((1, 128, 4, 4, 2, 64), np.float32)
    v_sample = np.zeros((1, 128, 4, 4, 128), np.float32)
    ret_sample = np.zeros((1, 128, 4, 64, 128), np.float32)
    for c in range(8):
        g, h = c // 4, c % 4
        y = np.asarray(RB[c]['y'])
        y_prompt[0, 1024 * c:1024 * (c + 1)] = y[0:1024]
        y_prompt[1, 1024 * c:1024 * (c + 1)] = y[1024:2048]
        srow = np.concatenate([(np.arange(8 * c, 8 * c + 8)[:, None] * 4 + np.arange(4)[None, :]).reshape(-1),
                               ((64 + np.arange(8 * c, 8 * c + 8))[:, None] * 4 + np.arange(4)[None, :]).reshape(-1)])
        y_sample[srow] = y[2048:2112]
        kv = np.asarray(RA[c]['kv'])
        k_prompt[0, g, :, h] = kv[0:8192, 0:128].reshape(8192, 2, 64)
        v_prompt[0, g, :, h] = kv[0:8192, 128:256]
        k_sample[0, 64 * g:64 * (g + 1), :, h] = kv[8192:8448, 0:128].reshape(64, 4, 2, 64)
        v_sample[0, 64 * g:64 * (g + 1), :, h] = kv[8192:8448, 128:256].reshape(64, 4, 128)
        ret_prompt[0, g, h] = np.asarray(RA[c]['rp'])
        ret_sample[0, 64 * g:64 * (g + 1), h] = np.asarray(RA[c]['rs'])
    return (y_prompt, y_sample.reshape(128, 4, 1024), k_prompt, v_prompt, ret_prompt, k_sample, v_sample, ret_sample)
```

```python
import numpy as np
import ml_dtypes
from contextlib import ExitStack
import concourse.bass as bass
import concourse.mybir as mybir
from concourse.bass_utils import run_bass_kernel_spmd

F32, BF16, I32 = mybir.dt.float32, mybir.dt.bfloat16, mybir.dt.int32
ALU = mybir.AluOpType
AF = mybir.ActivationFunctionType
AX = mybir.AxisListType

import os
NPH = int(os.environ.get('KDBG_NPH', '2560'))
GROUPS = [[0, 1, 2, 3], [4, 5, 6, 7], [8, 9, 10, 11], [12, 13, 14, 15, 16]]


class Prog:
    ENG = ('pe', 'act', 'dve', 'pool', 'sp')

    def __init__(self):
        self.items = {e: [] for e in self.ENG}
        self.cnt = {e: 0 for e in self.ENG}
        self.waited = {e: {} for e in self.ENG}
        self.buf = {}
        self.dcnt = {}
        self.sem_names = list(self.ENG)

    def _need(self, eng, tok, lst):
        if tok is None:
            return
        s, v = tok
        if s == eng and eng == 'pe':
            return
        if self.waited[eng].get(s, 0) >= v:
            return
        self.waited[eng][s] = v
        lst.append((s, v))

    def op(self, eng, fn, r=(), w=(), sig=True, dma=None, waits=()):
        wl = []
        for k in r:
            st = self.buf.get(k)
            if st:
                self._need(eng, st[0], wl)
        for k in w:
            st = self.buf.get(k)
            if st:
                self._need(eng, st[0], wl)
                for t in st[1]:
                    self._need(eng, t, wl)
        for t in waits:
            self._need(eng, t, wl)
        for (s, v) in wl:
            self.items[eng].append(('w', s, v))
        if dma is not None:
            sn = 'd_' + dma
            if sn not in self.dcnt:
                self.dcnt[sn] = 0
                self.sem_names.append(sn)
            self.dcnt[sn] += 16
            tok = (sn, self.dcnt[sn])
            self.items[eng].append(('d', fn, sn))
        else:
            if sig:
                self.cnt[eng] += 1
                tok = (eng, self.cnt[eng])
            else:
                tok = (eng, self.cnt[eng] + 1)
            self.items[eng].append(('o', fn, sig))
        for k in r:
            self.buf.setdefault(k, [None, []])[1].append(tok)
        for k in w:
            self.buf[k] = [tok, []]
        return tok

    def barrier(self):
        for e in self.ENG:
            wl = []
            for o in self.ENG:
                if o != e and self.cnt[o] > 0:
                    self._need(e, (o, self.cnt[o]), wl)
            for sn, v in self.dcnt.items():
                self._need(e, (sn, v), wl)
            for (s, v) in wl:
                self.items[e].append(('w', s, v))

    def emit(self, eng, e, sems):
        for it in self.items[eng]:
            if it[0] == 'w':
                e.wait_ge(sems[it[1]], it[2])
            elif it[0] == 'd':
                it[1](e).then_inc(sems[it[2]], 16)
            else:
                ins = it[1](e)
                if it[2]:
                    ins.then_inc(sems[eng], 1)
        self.items[eng] = []


def build_program(stage):
    nc = bass.Bass("TRN2", target_bir_lowering=False)
    D = {}

    def din(name, shape, dt):
        D[name] = nc.dram_tensor(name, shape, dt, kind="ExternalInput").ap()

    def dout(name, shape, dt):
        D[name] = nc.dram_tensor(name, shape, dt, kind="ExternalOutput").ap()

    if stage == 'A':
        din('xp', [8192, 1024], F32)
        din('xs', [256, 1024], F32)
        din('w_in', [128, 8 * 768], F32)
        din('gmix', [1, 1024], F32)
        din('gsl', [1, 128], F32)
        din('lamp', [1, 256], F32)
        din('ckv', [NPH * 8, 4096], F32)
        din('pt', [128, 64], I32)
        din('iota8', [128, 1], F32)
        din('st_ret', [64, 64, 128], F32)
        din('identb', [128, 128], BF16)
        din('identf', [128, 128], F32)
        din('maskT', [128, 128], BF16)
        din('cosd', [128, 66 * 8], F32)
        din('sind', [128, 66 * 8], F32)
        din('cosr', [128, 66 * 32], F32)
        din('sinr', [128, 66 * 32], F32)
        din('dsc', [128, 8], F32)
        din('iota', [128, 1], F32)
        din('smask', [128, 256], F32)
        din('rmask', [128, 128], F32)
        din('kmask', [128, 32 * 64], BF16)
        dout('kv', [8448, 256], F32)
        dout('rp', [64, 128], F32)
        dout('rs', [64, 64, 128], F32)
        dout('cat_p', [8192, 256], BF16)
        dout('cat_s', [256, 256], BF16)
        cat_p, cat_s = D['cat_p'], D['cat_s']
    else:
        din('xres', [2176, 1024], F32)
        din('w_out', [128, 8 * 1024], F32)
        din('wfi', [22, 128, 2048], F32)
        din('wfo', [22, 128, 1024], F32)
        din('gffn', [1, 1024], F32)
        din('gfin', [1, 1024], F32)
        din('identb', [128, 128], BF16)
        din('catin', [2176, 1024], BF16)
        dout('y', [2176, 1024], F32)

    P = Prog()
    op = P.op

    with ExitStack() as es0:
        BK = [es0.enter_context(nc.psum_tensor(f"bk{i}", [128, 512], F32)) for i in range(8)]
        sems = {}
        sem_pool = [es0.enter_context(nc.semaphore(f"sp{i}")) for i in range(96)]
        nc.all_engine_barrier()
        for sh in sem_pool:
            nc.gpsimd.sem_clear(sh)
        nc.all_engine_barrier()

        def run_block(body):
            body()
            P.barrier()
            for sn in P.sem_names:
                if sn not in sems:
                    sems[sn] = sem_pool[len(sems)]
            with nc.Block() as block:
                @block.tensor
                def _(e):
                    P.emit('pe', e, sems)

                @block.scalar
                def _(e):
                    P.emit('act', e, sems)

                @block.vector
                def _(e):
                    P.emit('dve', e, sems)

                @block.gpsimd
                def _(e):
                    P.emit('pool', e, sems)

                @block.sync
                def _(e):
                    P.emit('sp', e, sems)

        if stage == 'A':
            with ExitStack() as es, ExitStack() as es1:
                def sb(name, shape, dt, stack=None):
                    return (stack or es).enter_context(nc.sbuf_tensor('s_' + name, shape, dt))

                def sb1(name, shape, dt):
                    return sb(name, shape, dt, es1)

                identb = sb('identb', [128, 128], BF16)
                identf = sb('identf', [128, 128], F32)
                onesf = sb('onesf', [128, 128], F32)
                maskT = sb('maskT', [128, 128], BF16)
                cosd = sb('cosd', [128, 66 * 8], F32)
                sind = sb('sind', [128, 66 * 8], F32)
                cosr = sb('cosr', [128, 66 * 32], F32)
                sinr = sb('sinr', [128, 66 * 32], F32)
                dsc = sb('dsc', [128, 8], F32)
                iota = sb('iota', [128, 1], F32)
                iota8 = sb('iota8', [128, 1], F32)
                gmix = sb('gmixb', [128, 1024], F32)
                gsl = sb('gslb', [128, 128], F32)
                lp = sb('lp', [128, 256], F32)
                lsm = sb('lsm', [128, 8], F32)
                mhalf = sb('mhalf', [128, 1], F32)
                w_in = sb('w_in', [128, 8 * 768], BF16)
                x_sb = [sb(f'x{i}', [128, 1024], F32) for i in range(2)]
                xb = [sb(f'xb{i}', [128, 1024], BF16) for i in range(2)]
                xT = [sb(f'xT{i}', [128, 1024], BF16) for i in range(2)]
                junk = sb('junk', [128, 1024], BF16)
                stat = sb('stat', [128, 16], F32)
                z_sb = [sb(f'z{i}', [128, 768], F32) for i in range(2)]
                rt = sb('rt', [128, 4 * 32], F32)
                rtr = sb('rtr', [128, 4 * 64], F32)
                zb_qk = [sb(f'zbqk{i}', [128, 256], BF16) for i in range(2)]
                zb_r = [sb(f'zbr{i}', [128, 256], BF16) for i in range(2)]
                eg = [sb(f'eg{i}', [128, 128], F32) for i in range(2)]
                sg = sb('sg', [128, 8 * 128], F32)
                ro_sb = sb('ro_sb', [128, 128], F32)
                o_sb = sb('o_sb', [128, 128], F32)
                otmp = sb('otmp', [128, 128], F32)
                ep = sb('ep', [128, 16], F32)
                cat_st = [sb(f'cat{i}', [128, 256], BF16) for i in range(8)]
                KT = sb1('KT', [128, 8192], BF16)
                Vext = sb1('Vext', [128, 64 * 130], BF16)
                QT = [sb1(f'QT{i}', [128, 512], BF16) for i in range(2)]
                RT = [sb1(f'RT{i}', [64, 256], BF16) for i in range(2)]
                PT = [sb1(f'PT{i}', [128, 512], BF16) for i in range(4)]
                AT = sb1('AT', [128, 128], BF16)
                wst = sb1('wst', [64, 128], F32)
                st_bf = sb1('st_bf', [64, 128], BF16)
                rp_sb = sb1('rp_sb', [64, 128], F32)
                QTs = KTs = RTs = Qblk = ptb = idxa = kv_sb = KTp = PTs = Rsum = PN = smask = rmask = kmask = On = dT = roT = st_sb = st_bfs = ATs = Kpad = None
                def alloc_sample():
                    nonlocal QTs, KTs, RTs, Qblk, ptb, idxa, kv_sb, KTp, PTs, Rsum, PN, smask, rmask, kmask, On, dT, roT, st_sb, st_bfs, ATs, Kpad
                    QTs = sb('QTs', [128, 256], BF16)
                    KTs = sb('KTs', [128, 256], BF16)
                    RTs = [sb(f'RTs{i}', [64, 256], BF16) for i in range(2)]
                    Qblk = sb('Qblk', [128, 512], BF16)
                    ptb = sb('ptb', [128, 64], I32)
                    idxa = sb('idxa', [128, 64], I32)
                    kv_sb = [sb(f'kvs{i}', [128, 16 * 256], F32) for i in range(2)]
                    KTp = [sb(f'KTp{i}', [128, 2048], BF16) for i in range(2)]
                    PTs = [sb(f'PTs{i}', [128, 128], F32) for i in range(2)]
                    Rsum = sb('Rsum', [128, 512], F32)
                    PN = [sb(f'PN{i}', [128, 256], F32) for i in range(2)]
                    smask = sb('smask', [128, 256], F32)
                    rmask = sb('rmask', [128, 128], F32)
                    kmask = sb('kmask', [128, 2048], BF16)
                    On = sb('On', [128, 512], F32)
                    dT = sb('dT', [128, 256], F32)
                    roT = sb('roT', [128, 256], F32)
                    st_sb = sb('st_sb', [64, 64 * 128], F32)
                    st_bfs = sb('st_bfs', [64, 64 * 128], BF16)
                    ATs = [sb(f'ATs{i}', [128, 128], BF16) for i in range(2)]
                    Kpad = [sb(f'Kpad{i}', [128, 2048], BF16) for i in range(2)]

                def stage_a(part):
                    def ld(dst, src, key, q='sp'):
                        op(q, lambda e, dst=dst, src=src: e.dma_start(out=dst, in_=src), w=[key], dma='c_' + str(key))
                    nlam = lsm[:, 5:6]
                    if part == 1:
                        ld(identb[:], D['identb'][:, :], 'identb')
                        ld(identf[:], D['identf'][:, :], 'identf')
                        ld(maskT[:], D['maskT'][:, :], 'maskT')
                        ld(cosd[:], D['cosd'][:, :], 'cosd')
                        ld(sind[:], D['sind'][:, :], 'sind')
                        ld(cosr[:], D['cosr'][:, :], 'cosr')
                        ld(sinr[:], D['sinr'][:, :], 'sinr')
                        ld(dsc[:], D['dsc'][:, :], 'dsc')
                        ld(iota[:], D['iota'][:, :], 'iota')
                        ld(gmix[:], D['gmix'][0:1, :].broadcast_to([128, 1024]), 'gmix')
                        ld(gsl[:], D['gsl'][0:1, :].broadcast_to([128, 128]), 'gsl')
                        ld(lp[:], D['lamp'][0:1, :].broadcast_to([128, 256]), 'lp')
                        for kc in range(8):
                            op('pool', lambda e, kc=kc: e.dma_start(out=w_in[:, kc * 768:(kc + 1) * 768],
                                                                   in_=D['w_in'][:, kc * 768:(kc + 1) * 768]),
                               w=[('win', kc)], dma=f'win{kc}')
                        op('dve', lambda e: e.memset(mhalf[:], -0.5), w=['mhalf'])
                        op('dve', lambda e: e.memset(onesf[:], 1.0), w=['onesf'])
                        op('dve', lambda e: e.memset(wst[:], 0.0), w=['wst'])
                        op('dve', lambda e: e.memset(st_bf[:], 0.0), w=['st_bf'])
                        op('pool', lambda e: e.memset(Vext[:].rearrange("p (t c) -> p t c", c=130)[:, :, 128:130], 1.0), w=['Vones'])
                        op('dve', lambda e: e.tensor_scalar(out=gsl[:], in0=gsl[:], scalar1=0.8, scalar2=None, op0=ALU.mult),
                           r=['gsl'], w=['gsl'])
                        op('dve', lambda e: e.tensor_tensor(out=junk[:, 0:64], in0=lp[:, 0:64], in1=lp[:, 64:128], op=ALU.mult), r=['lp'], w=['junk'])
                        op('dve', lambda e: e.tensor_reduce(out=lsm[:, 0:1], in_=junk[:, 0:64], axis=AX.X, op=ALU.add), r=['junk'], w=['lsm0'])
                        op('dve', lambda e: e.tensor_tensor(out=junk[:, 0:64], in0=lp[:, 128:192], in1=lp[:, 192:256], op=ALU.mult), r=['lp'], w=['junk'])
                        op('dve', lambda e: e.tensor_reduce(out=lsm[:, 1:2], in_=junk[:, 0:64], axis=AX.X, op=ALU.add), r=['junk'], w=['lsm1'])
                        op('act', lambda e: e.activation(out=lsm[:, 2:4], in_=lsm[:, 0:2], func=AF.Exp), r=['lsm0', 'lsm1'], w=['lsm2'])
                        op('dve', lambda e: e.tensor_tensor(out=lsm[:, 4:5], in0=lsm[:, 3:4], in1=lsm[:, 2:3], op=ALU.subtract),
                           r=['lsm2'], w=['lsm4'])
                        op('dve', lambda e: e.tensor_scalar(out=lsm[:, 5:6], in0=lsm[:, 4:5], scalar1=-0.2, scalar2=None, op0=ALU.add),
                           r=['lsm4'], w=['nlam'])

                    def rstd_chain(ssap, vap, outap, n, eps, keyin, keyout):
                        op('dve', lambda e: e.tensor_scalar(out=vap, in0=ssap, scalar1=1.0 / n, scalar2=eps,
                                                            op0=ALU.mult, op1=ALU.add), r=[keyin], w=[keyout + '_v'])
                        op('act', lambda e: e.activation(out=outap, in_=vap, func=AF.Ln), r=[keyout + '_v'], w=[keyout])
                        op('act', lambda e: e.activation(out=outap, in_=outap, func=AF.Exp, scale=-0.5), r=[keyout], w=[keyout])
                        for _ in range(1):
                            op('dve', lambda e: e.scalar_tensor_tensor(out=ssap, in0=outap, scalar=vap, in1=outap,
                                                                       op0=ALU.mult, op1=ALU.mult), r=[keyout, keyout + '_v'], w=[keyin])
                            op('dve', lambda e: e.tensor_scalar(out=ssap, in0=ssap, scalar1=-0.5, scalar2=1.5,
                                                                op0=ALU.mult, op1=ALU.add), r=[keyin], w=[keyin])
                            op('dve', lambda e: e.tensor_tensor(out=outap, in0=outap, in1=ssap, op=ALU.mult),
                               r=[keyin, keyout], w=[keyout])

                    def proj_tile(t):
                        sl = t % 2
                        samp = t >= 64
                        ts = t - 64
                        src = D['xs'][ts * 128:(ts + 1) * 128, :] if samp else D['xp'][t * 128:(t + 1) * 128, :]
                        op('sp', lambda e: e.dma_start(out=x_sb[sl][:], in_=src), w=[('x', sl)], dma=f'x{sl}')
                        op('dve', lambda e: e.tensor_tensor(out=junk[:], in0=x_sb[sl][:], in1=x_sb[sl][:], op=ALU.mult), r=[('x', sl)], w=['junk'])
                        op('dve', lambda e: e.tensor_reduce(out=stat[:, sl:sl + 1], in_=junk[:], axis=AX.X, op=ALU.add), r=['junk'], w=[('ss', sl)])
                        rstd_chain(stat[:, sl:sl + 1], stat[:, 2 + sl:3 + sl], stat[:, 4 + sl:5 + sl], 1024.0, 1e-6,
                                   ('ss', sl), f'rstd{sl}')
                        op('dve', lambda e: e.scalar_tensor_tensor(out=xb[sl][:], in0=x_sb[sl][:], scalar=stat[:, 4 + sl:5 + sl],
                                                                   in1=gmix[:], op0=ALU.mult, op1=ALU.mult),
                           r=[('x', sl), f'rstd{sl}', 'gmix'], w=[('xb', sl)])
                        for half in range(2):
                            for k in range(4):
                                kc = half * 4 + k
                                op('pe', lambda e, k=k, kc=kc: e.matmul(BK[0][:, k * 128:(k + 1) * 128],
                                                                       lhsT=xb[sl][:, kc * 128:(kc + 1) * 128], rhs=identb[:],
                                                                       start=True, stop=True),
                                   r=[('xb', sl), 'identb'], w=['bk0'] if k == 0 else [], sig=(k == 3))
                            if half == 0:
                                op('act', lambda e: e.activation(out=xT[sl][:, 0:512], in_=BK[0][:, :], func=AF.Copy),
                                   r=['bk0'], w=[('xT', sl, 0)])
                            else:
                                op('dve', lambda e: e.tensor_copy(out=xT[sl][:, 512:1024], in_=BK[0][:, :]),
                                   r=['bk0'], w=[('xT', sl, 1)])
                        for kc in range(8):
                            op('pe', lambda e, kc=kc: e.matmul(BK[1][:, :], lhsT=xT[sl][:, kc * 128:(kc + 1) * 128],
                                                               rhs=w_in[:, kc * 768:kc * 768 + 512], start=(kc == 0), stop=(kc == 7)),
                               r=[('xT', sl, 0), ('xT', sl, 1), ('win', kc)], w=['bk1'] if kc == 0 else [], sig=False)
                            op('pe', lambda e, kc=kc: e.matmul(BK[2][:, 0:256], lhsT=xT[sl][:, kc * 128:(kc + 1) * 128],
                                                               rhs=w_in[:, kc * 768 + 512:kc * 768 + 768], start=(kc == 0), stop=(kc == 7)),
                               r=[], w=['bk2z'] if kc == 0 else [], sig=(kc == 7))
                        zs = z_sb[sl]
                        op('act', lambda e: e.activation(out=zs[:, 0:512], in_=BK[1][:, :], func=AF.Copy),
                           r=['bk1'], w=[('zqk', sl), ('zv', sl), ('zr', sl)])
                        op('dve', lambda e: e.tensor_copy(out=zs[:, 512:768], in_=BK[2][:, 0:256]),
                           r=['bk2z'], w=[('zb', sl)])
                        zq = zs[:, 0:256].rearrange("p (g d) -> p g d", d=64)
                        x1, x2 = zq[:, :, 0:8], zq[:, :, 8:16]
                        cd = cosd[:, t * 8:(t + 1) * 8].unsqueeze(1).broadcast_to([128, 4, 8])
                        sd = sind[:, t * 8:(t + 1) * 8].unsqueeze(1).broadcast_to([128, 4, 8])
                        rtv = rt[:].rearrange("p (a g d) -> p a g d", a=4, g=4)
                        for i, (a, b) in enumerate([(x1, cd), (x2, sd), (x2, cd), (x1, sd)]):
                            op('dve', lambda e, i=i, a=a, b=b: e.tensor_tensor(out=rtv[:, i], in0=a, in1=b, op=ALU.mult),
                               r=[('zqk', sl), 'cosd', 'sind'], w=[('rt', i)])
                        op('dve', lambda e: e.tensor_tensor(out=x1, in0=rtv[:, 0], in1=rtv[:, 1], op=ALU.subtract),
                           r=[('rt', 0), ('rt', 1)], w=[('zqk', sl)])
                        op('dve', lambda e: e.tensor_tensor(out=x2, in0=rtv[:, 2], in1=rtv[:, 3], op=ALU.add),
                           r=[('rt', 2), ('rt', 3)], w=[('zqk', sl)])
                        c0 = 2 if samp else 0
                        for qi in range(2):
                            col = 384 + qi * 64
                            op('dve', lambda e, col=col, qi=qi: e.tensor_scalar(out=zs[:, col:col + 64], in0=zs[:, col:col + 64],
                                                                                 scalar1=dsc[:, c0 + qi:c0 + qi + 1], scalar2=None,
                                                                                 op0=ALU.mult),
                               r=[('zr', sl), 'dsc'], w=[('zr', sl)])
                        zr = zs[:, 384:512].rearrange("p (g d) -> p g d", d=64)
                        y1, y2 = zr[:, :, 0:32], zr[:, :, 32:64]
                        cr = cosr[:, t * 32:(t + 1) * 32].unsqueeze(1).broadcast_to([128, 2, 32])
                        sr = sinr[:, t * 32:(t + 1) * 32].unsqueeze(1).broadcast_to([128, 2, 32])
                        rrv = rtr[:].rearrange("p (a g d) -> p a g d", a=4, g=2)
                        for i, (a, b) in enumerate([(y1, cr), (y2, sr), (y2, cr), (y1, sr)]):
                            op('dve', lambda e, i=i, a=a, b=b: e.tensor_tensor(out=rrv[:, i], in0=a, in1=b, op=ALU.mult),
                               r=[('zr', sl), 'cosr', 'sinr'], w=[('rtr', i)])
                        op('dve', lambda e: e.tensor_tensor(out=y1, in0=rrv[:, 0], in1=rrv[:, 1], op=ALU.subtract),
                           r=[('rtr', 0), ('rtr', 1)], w=[('zr', sl)])
                        op('dve', lambda e: e.tensor_tensor(out=y2, in0=rrv[:, 2], in1=rrv[:, 3], op=ALU.add),
                           r=[('rtr', 2), ('rtr', 3)], w=[('zr', sl)])
                        kvtok = op('sp', lambda e: e.dma_start(out=D['kv'][t * 128:(t + 1) * 128, :], in_=zs[:, 128:384]),
                                   r=[('zqk', sl), ('zv', sl)], dma=f'kvo{sl}')
                        op('act', lambda e: e.activation(out=zb_qk[sl][:], in_=zs[:, 0:256], func=AF.Copy),
                           r=[('zqk', sl)], w=[('zbqk', sl)])
                        if not samp:
                            vx = Vext[:, t * 130:t * 130 + 128]
                            op('act', lambda e: e.activation(out=vx, in_=zs[:, 256:384], func=AF.Copy), r=[('zv', sl)], w=[('V', t)])
                        op('act', lambda e: e.activation(out=zb_r[sl][:], in_=zs[:, 384:640], func=AF.Copy),
                           r=[('zr', sl), ('zb', sl)], w=[('zbr', sl)])
                        s8 = t % 8
                        op('act', lambda e: e.activation(out=eg[sl][:], in_=zs[:, 640:768], func=AF.Exp, scale=-1.0),
                           r=[('zb', sl)], w=[('eg', sl)])
                        op('dve', lambda e: e.tensor_scalar(out=eg[sl][:], in0=eg[sl][:], scalar1=1.0, scalar2=None, op0=ALU.add),
                           r=[('eg', sl)], w=[('eg', sl)])
                        op('dve', lambda e: e.reciprocal(out=eg[sl][:], in_=eg[sl][:]), r=[('eg', sl)], w=[('eg', sl)])
                        op('dve', lambda e: e.tensor_tensor(out=sg[:, s8 * 128:(s8 + 1) * 128], in0=eg[sl][:], in1=zs[:, 640:768],
                                                             op=ALU.mult), r=[('eg', sl), ('zb', sl)], w=[('sg', s8)])
                        op('pe', lambda e: e.matmul(BK[0][:, 0:128], lhsT=zb_qk[sl][:, 0:128], rhs=identb[:], start=True, stop=True),
                           r=[('zbqk', sl)], w=['bk0'], sig=False)
                        op('pe', lambda e: e.matmul(BK[0][:, 128:256], lhsT=zb_qk[sl][:, 128:256], rhs=identb[:], start=True, stop=True),
                           sig=False)
                        op('pe', lambda e: e.matmul(BK[0][0:64, 256:384], lhsT=zb_r[sl][:, 0:64], rhs=identb[:], start=True, stop=True),
                           r=[('zbr', sl)], sig=False)
                        op('pe', lambda e: e.matmul(BK[0][0:64, 384:512], lhsT=zb_r[sl][:, 64:128], rhs=identb[:], start=True, stop=True),
                           sig=True)
                        if samp:
                            qdst, qk = QTs[:, ts * 128:(ts + 1) * 128], ('QTs', ts)
                            kdst, kk = KTs[:, ts * 128:(ts + 1) * 128], ('KTs', ts)
                            rdst, rk = RTs[ts][:], ('RTs', ts)
                        else:
                            I, qs = t // 4, t % 4
                            qdst, qk = QT[I % 2][:, qs * 128:(qs + 1) * 128], ('QT', I % 2, qs)
                            kdst, kk = KT[:, t * 128:(t + 1) * 128], ('KT', t)
                            rdst, rk = RT[sl][:], ('RT', sl)
                        op('act', lambda e: e.activation(out=qdst, in_=BK[0][:, 0:128], func=AF.Copy), r=['bk0'], w=[qk])
                        op('dve', lambda e: e.tensor_copy(out=kdst, in_=BK[0][:, 128:256]), r=['bk0'], w=[kk])
                        op('dve', lambda e: e.tensor_copy(out=rdst, in_=BK[0][0:64, 256:512]), r=['bk0'], w=[rk])
                        return kvtok

                    def ret_epilogue(src_ps, srckey, s8, cslot):
                        op('act', lambda e: e.activation(out=ro_sb[:], in_=src_ps, func=AF.Copy), r=[srckey], w=['ro_sb'])
                        op('dve', lambda e: e.tensor_tensor(out=junk[:, 0:128], in0=ro_sb[:], in1=ro_sb[:], op=ALU.mult), r=['ro_sb'], w=['junk'])
                        op('dve', lambda e: e.tensor_reduce(out=ep[:, 8:9], in_=junk[:, 0:128], axis=AX.X, op=ALU.add), r=['junk'], w=['ep8'])
                        rstd_chain(ep[:, 8:9], ep[:, 9:10], ep[:, 10:11], 128.0, 1e-6, 'ep8', 'rrstd')
                        op('dve', lambda e: e.scalar_tensor_tensor(out=cat_st[cslot][:, 128:256], in0=ro_sb[:], scalar=ep[:, 10:11],
                                                                   in1=sg[:, s8 * 128:(s8 + 1) * 128], op0=ALU.mult, op1=ALU.mult),
                           r=['ro_sb', 'rrstd', ('sg', s8)], w=[('catr', cslot)])

                    def diff_epilogue(cslot):
                        op('dve', lambda e: e.tensor_tensor(out=junk[:, 0:128], in0=o_sb[:], in1=o_sb[:], op=ALU.mult), r=['o_sb'], w=['junk'])
                        op('dve', lambda e: e.tensor_reduce(out=ep[:, 3:4], in_=junk[:, 0:128], axis=AX.X, op=ALU.add), r=['junk'], w=['ep3'])
                        rstd_chain(ep[:, 3:4], ep[:, 4:5], ep[:, 5:6], 128.0, 1e-5, 'ep3', 'orstd')
                        op('dve', lambda e: e.scalar_tensor_tensor(out=cat_st[cslot][:, 0:128], in0=o_sb[:], scalar=ep[:, 5:6],
                                                                   in1=gsl[:], op0=ALU.mult, op1=ALU.mult),
                           r=['o_sb', 'orstd', 'gsl'], w=[('catd', cslot)])

                    def retention(t):
                        sl = t % 2
                        op('pe', lambda e: e.matmul(BK[2][:, 256:384], lhsT=RT[sl][0:64, 128:256], rhs=RT[sl][0:64, 0:128],
                                                    start=True, stop=True), r=[('RT', sl)], w=['bk2s'])
                        op('dve', lambda e: e.tensor_tensor(out=AT[:], in0=BK[2][:, 256:384], in1=maskT[:], op=ALU.mult),
                           r=['bk2s', 'maskT'], w=['AT'])
                        op('pe', lambda e: e.matmul(BK[2][:, 384:512], lhsT=AT[:], rhs=zb_r[sl][:, 128:256], start=True, stop=False),
                           r=['AT', ('zbr', sl)], w=['bk2o'], sig=False)
                        op('pe', lambda e: e.matmul(BK[2][:, 384:512], lhsT=RT[sl][0:64, 0:128], rhs=st_bf[:], start=False, stop=True),
                           r=['st_bf', ('RT', sl)], sig=True)
                        op('pe', lambda e: e.matmul(BK[2][0:64, 256:384], lhsT=zb_r[sl][:, 64:128], rhs=zb_r[sl][:, 128:256],
                                                    start=True, stop=True), r=[('zbr', sl)], w=['bk2s'])
                        op('dve', lambda e: e.scalar_tensor_tensor(out=wst[:], in0=wst[:], scalar=dsc[0:64, 4:5],
                                                                   in1=BK[2][0:64, 256:384], op0=ALU.mult, op1=ALU.add),
                           r=['bk2s', 'dsc'], w=['wst'])
                        op('act', lambda e: e.activation(out=st_bf[:], in_=wst[:], func=AF.Copy, scale=dsc[0:64, 4:5]),
                           r=['wst', 'dsc'], w=['st_bf'])
                        ret_epilogue(BK[2][:, 384:512], 'bk2o', t % 8, t % 8)

                    def attention(I, nxt=()):
                        started = set()
                        qt = QT[I % 2]
                        nkb = 4 * I + 4
                        pti = [0]
                        nxt = list(nxt)
                        step = max(1, nkb // 4)
                        for kb in range(nkb):
                            if nxt and kb % step == 0 and kb // step < 4 and len(nxt) == 4 - kb // step:
                                tn = nxt.pop(0)
                                proj_tile(tn)
                                retention(tn)
                            r = kb - 4 * I
                            q0 = max(r, 0) * 128
                            for m in range(2):
                                sbk = BK[3 + m]
                                op('pe', lambda e, m=m, kb=kb, q0=q0, sbk=sbk: e.matmul(
                                    sbk[:, q0:512], lhsT=KT[m * 64:(m + 1) * 64, kb * 128:(kb + 1) * 128],
                                    rhs=qt[m * 64:(m + 1) * 64, q0:512], start=True, stop=True),
                                   r=[('KT', kb)] + [('QT', I % 2, q) for q in range(4)], w=[('S', m)])
                                ps = pti[0] % 4
                                pti[0] += 1
                                ptt = PT[ps]
                                op('act', lambda e, q0=q0, sbk=sbk, ptt=ptt: e.activation(out=ptt[:, q0:512], in_=sbk[:, q0:512],
                                                                                          func=AF.Exp, scale=0.125),
                                   r=[('S', m)], w=[('PT', ps)])
                                if r >= 0:
                                    op('dve', lambda e, r=r, ptt=ptt: e.tensor_tensor(out=ptt[:, r * 128:(r + 1) * 128],
                                                                                       in0=ptt[:, r * 128:(r + 1) * 128],
                                                                                       in1=maskT[:], op=ALU.mult),
                                       r=[('PT', ps), 'maskT'], w=[('PT', ps)])
                                for qs in range(max(r, 0), 4):
                                    a = m * 4 + qs
                                    bank = 5 + a // 3
                                    c0 = (a % 3) * 129
                                    st = bank not in started
                                    started.add(bank)
                                    last = (kb == 4 * I + qs)
                                    op('pe', lambda e, bank=bank, c0=c0, qs=qs, kb=kb, st=st, last=last, ptt=ptt: e.matmul(
                                        BK[bank][:, c0:c0 + 129], lhsT=ptt[:, qs * 128:(qs + 1) * 128],
                                        rhs=Vext[:, kb * 130:kb * 130 + 129], start=st, stop=last, skip_group_check=True),
                                       r=[('PT', ps), ('V', kb), 'Vones'], w=[('acc', a)], sig=last)
                        for tn in nxt:
                            proj_tile(tn)
                            retention(tn)
                        for qs in range(4):
                            t = 4 * I + qs
                            a0, a1 = qs, 4 + qs
                            A0 = BK[5 + a0 // 3][:, (a0 % 3) * 129:(a0 % 3) * 129 + 129]
                            A1 = BK[5 + a1 // 3][:, (a1 % 3) * 129:(a1 % 3) * 129 + 129]
                            op('dve', lambda e, A0=A0: e.reciprocal(out=ep[:, 0:1], in_=A0[:, 128:129]), r=[('acc', a0)], w=['ep0'])
                            op('dve', lambda e, A1=A1: e.reciprocal(out=ep[:, 1:2], in_=A1[:, 128:129]), r=[('acc', a1)], w=['ep1'])
                            op('dve', lambda e: e.tensor_tensor(out=ep[:, 2:3], in0=ep[:, 1:2], in1=nlam, op=ALU.mult),
                               r=['ep1', 'nlam'], w=['ep2'])
                            op('dve', lambda e, A1=A1: e.tensor_scalar(out=otmp[:], in0=A1[:, 0:128], scalar1=ep[:, 2:3], scalar2=None,
                                                                       op0=ALU.mult), r=['ep2', ('acc', a1)], w=['otmp'])
                            op('dve', lambda e, A0=A0: e.scalar_tensor_tensor(out=o_sb[:], in0=A0[:, 0:128], scalar=ep[:, 0:1],
                                                                              in1=otmp[:], op0=ALU.mult, op1=ALU.add),
                               r=['ep0', 'otmp', ('acc', a0)], w=['o_sb'])
                            diff_epilogue(t % 8)

                    if part == 1:
                        cat_toks = []
                        for qs in range(4):
                            proj_tile(qs)
                            retention(qs)
                        for I in range(16):
                            nxt = [4 * (I + 1) + q for q in range(4)] if I < 15 else []
                            attention(I, nxt)
                            for qs in range(4):
                                t = 4 * I + qs
                                cs = t % 8
                                cat_toks.append(op('sp', lambda e, t=t, cs=cs: e.dma_start(out=cat_p[t * 128:(t + 1) * 128, :],
                                                                                           in_=cat_st[cs][:]),
                                                   r=[('catd', cs), ('catr', cs)], dma=f'cat{cs}'))
                        op('dve', lambda e: e.tensor_scalar(out=rp_sb[:], in0=wst[:], scalar1=dsc[0:64, 4:5], scalar2=None, op0=ALU.mult),
                           r=['wst', 'dsc'], w=['rp_sb'])
                        op('sp', lambda e: e.dma_start(out=D['rp'][:, :], in_=rp_sb[:]), r=['rp_sb'], dma='rp')

                        return
                    ld(smask[:], D['smask'][:, :], 'smask')
                    ld(rmask[:], D['rmask'][:, :], 'rmask')
                    ld(kmask[:], D['kmask'][:, :], 'kmask')
                    ld(ptb[:], D['pt'][:, :], 'ptb')
                    ld(iota8[:], D['iota8'][:, :], 'iota8')
                    ld(st_sb[:].rearrange("d (b e) -> d b e", e=128), D['st_ret'].rearrange("b d e -> d b e"), 'st_sb')
                    op('pool', lambda e: e.memset(Qblk[:], 0.0), w=['Qblk'])
                    op('dve', lambda e: e.tensor_scalar(out=idxa[:], in0=ptb[:], scalar1=8.0, scalar2=iota8[:, 0:1],
                                                        op0=ALU.mult, op1=ALU.add), r=['ptb', 'iota8'], w=['idxa'])
                    proj_tile(64)
                    proj_tile(65)
                    for m in range(2):
                        qv = Qblk[m * 64:(m + 1) * 64, :].rearrange("p (b x) -> p b x", x=8)[:, :, m * 4:(m + 1) * 4]
                        sv = QTs[m * 64:(m + 1) * 64, :].rearrange("p (b q) -> p b q", q=4)
                        op('dve', lambda e, qv=qv, sv=sv: e.tensor_copy(out=qv, in_=sv), r=[('QTs', 0), ('QTs', 1), 'Qblk'],
                           w=[('Qb', m)])
                    OT = BK[7]
                    for kt in range(2):
                        op('pe', lambda e, kt=kt: e.matmul(BK[1][:, kt * 256:(kt + 1) * 256], lhsT=KTs[:, kt * 128:(kt + 1) * 128],
                                                           rhs=Qblk[:, kt * 256:(kt + 1) * 256], start=True, stop=True),
                           r=[('KTs', kt), ('Qb', 0), ('Qb', 1)], w=[('SN', kt)])
                        op('act', lambda e, kt=kt: e.activation(out=PN[kt][:], in_=BK[1][:, kt * 256:(kt + 1) * 256], func=AF.Exp,
                                                                scale=0.125), r=[('SN', kt)], w=[('PN', kt)])
                        op('dve', lambda e, kt=kt: e.tensor_tensor(out=PN[kt][:], in0=PN[kt][:], in1=smask[:], op=ALU.mult),
                           r=[('PN', kt), 'smask'], w=[('PN', kt)])
                        op('pe', lambda e, kt=kt: e.matmul(OT[:, kt * 256:(kt + 1) * 256], lhsT=z_sb[kt][:, 256:384], rhs=PN[kt][:],
                                                           start=(kt == 0), stop=False, skip_group_check=True),
                           r=[('PN', kt), ('zv', kt)], w=['OT'] if kt == 0 else [], sig=False)
                        op('pe', lambda e, kt=kt: e.matmul(BK[0][:, kt * 256:(kt + 1) * 256], lhsT=onesf[:], rhs=PN[kt][:],
                                                           start=True, stop=True), r=['onesf'], w=[('RN', kt)])
                    for b in range(64):
                        bs = b % 2
                        s4 = b % 4
                        op('pool', lambda e, b=b, bs=bs: e.indirect_dma_start(
                            out=kv_sb[bs][:, :], out_offset=None, in_=D['ckv'][:, :],
                            in_offset=bass.IndirectOffsetOnAxis(ap=idxa[:, b:b + 1], axis=0)),
                           r=['idxa'], w=[('kvs', bs)], dma=f'kvs{bs}')
                        for g4 in range(4):
                            tb = BK[3 + g4 % 2]
                            for k in range(4):
                                pg = g4 * 4 + k
                                op('pe', lambda e, k=k, pg=pg, tb=tb, bs=bs: e.transpose(tb[:, k * 128:(k + 1) * 128],
                                                                                         kv_sb[bs][:, pg * 256:pg * 256 + 128],
                                                                                         identf[:]),
                                   r=[('kvs', bs), 'identf'], w=[('TB', g4 % 2)] if k == 0 else [], sig=(k == 3))
                            if g4 % 2 == 0:
                                op('act', lambda e, g4=g4, tb=tb, bs=bs: e.activation(out=KTp[bs][:, g4 * 512:(g4 + 1) * 512],
                                                                                      in_=tb[:, :], func=AF.Copy),
                                   r=[('TB', g4 % 2)], w=[('KTp', bs, g4)])
                            else:
                                op('dve', lambda e, g4=g4, tb=tb, bs=bs: e.tensor_copy(out=KTp[bs][:, g4 * 512:(g4 + 1) * 512],
                                                                                       in_=tb[:, :]),
                                   r=[('TB', g4 % 2)], w=[('KTp', bs, g4)])
                        for pg in range(16):
                            op('pe', lambda e, pg=pg, b=b, bs=bs, s4=s4: e.matmul(
                                BK[5][:, s4 * 128 + pg * 8:s4 * 128 + pg * 8 + 8], lhsT=KTp[bs][:, pg * 128:(pg + 1) * 128],
                                rhs=Qblk[:, b * 8:(b + 1) * 8], start=True, stop=True),
                               r=[('KTp', bs, pg // 4), ('Qb', 0), ('Qb', 1)], w=[('SB', s4)] if pg == 0 else [], sig=(pg == 15))
                        op('act', lambda e, bs=bs, s4=s4: e.activation(out=PTs[bs][:], in_=BK[5][:, s4 * 128:(s4 + 1) * 128],
                                                                       func=AF.Exp, scale=0.125), r=[('SB', s4)], w=[('PTs', bs)])
                        op('pe', lambda e, bs=bs, s4=s4: e.matmul(BK[6][:, s4 * 128:(s4 + 1) * 128], lhsT=onesf[:], rhs=PTs[bs][:],
                                                                  start=True, stop=True), r=[('PTs', bs), 'onesf'], w=[('RSB', s4)])
                        op('dve', lambda e, b=b, s4=s4: e.tensor_reduce(
                            out=Rsum[:, b * 8:(b + 1) * 8],
                            in_=BK[6][:, s4 * 128:(s4 + 1) * 128].rearrange("p (g c) -> p c g", c=8), axis=AX.X, op=ALU.add),
                           r=[('RSB', s4)], w=[('Rsum', b)])
                        for pg in range(16):
                            op('pe', lambda e, pg=pg, b=b, bs=bs: e.matmul(
                                OT[:, b * 8:(b + 1) * 8], lhsT=kv_sb[bs][:, pg * 256 + 128:pg * 256 + 256],
                                rhs=PTs[bs][:, pg * 8:(pg + 1) * 8], start=False, stop=(b == 63 and pg == 15), skip_group_check=True),
                               r=[('PTs', bs), ('kvs', bs)], w=['OT'] if (b == 63 and pg == 15) else [], sig=(pg == 15))
                    op('dve', lambda e: e.tensor_tensor(out=Rsum[:], in0=Rsum[:], in1=BK[0][:, :], op=ALU.add),
                       r=[('Rsum', b) for b in range(64)] + [('RN', 0), ('RN', 1)], w=['Rtot'])
                    op('dve', lambda e: e.reciprocal(out=Rsum[:], in_=Rsum[:]), r=['Rtot'], w=['Rtot'])
                    op('dve', lambda e: e.tensor_tensor(out=On[:], in0=OT[:, :], in1=Rsum[:], op=ALU.mult), r=['Rtot', 'OT'], w=['On'])
                    Onv = On[:].rearrange("p (b m q) -> p b m q", m=2, q=4)
                    op('dve', lambda e: e.scalar_tensor_tensor(out=dT[:].rearrange("p (b q) -> p b q", q=4), in0=Onv[:, :, 1, :],
                                                               scalar=nlam, in1=Onv[:, :, 0, :], op0=ALU.mult, op1=ALU.add),
                       r=['On', 'nlam'], w=['dT'])
                    for tt in range(2):
                        op('pe', lambda e, tt=tt: e.transpose(BK[3][:, tt * 128:(tt + 1) * 128], dT[:, tt * 128:(tt + 1) * 128],
                                                              identf[:]), r=['dT', 'identf'], w=[('dTt', tt)])
                    for tt in range(2):
                        op('act', lambda e, tt=tt: e.activation(out=o_sb[:], in_=BK[3][:, tt * 128:(tt + 1) * 128], func=AF.Copy),
                           r=[('dTt', tt)], w=['o_sb'])
                        diff_epilogue(tt)
                    op('act', lambda e: e.activation(out=st_bfs[:], in_=st_sb[:], func=AF.Copy), r=['st_sb'], w=['st_bfs'])
                    op('dve', lambda e: e.tensor_scalar(out=st_sb[:], in0=st_sb[:], scalar1=dsc[0:64, 5:6], scalar2=None, op0=ALU.mult),
                       r=['st_sb', 'dsc'], w=['st_sb'])
                    for kt in range(2):
                        op('pe', lambda e, kt=kt: e.matmul(BK[2][:, 256 + kt * 128:384 + kt * 128], lhsT=RTs[kt][0:64, 128:256],
                                                           rhs=RTs[kt][0:64, 0:128], start=True, stop=True),
                           r=[('RTs', kt)], w=[('SRs', kt)])
                        op('dve', lambda e, kt=kt: e.tensor_tensor(out=ATs[kt][:], in0=BK[2][:, 256 + kt * 128:384 + kt * 128],
                                                                   in1=rmask[:], op=ALU.mult), r=[('SRs', kt), 'rmask'], w=[('ATs', kt)])
                        op('dve', lambda e, kt=kt: e.tensor_tensor(
                            out=Kpad[kt][:].rearrange("p (b d) -> p b d", d=64),
                            in0=zb_r[kt][:, 64:128].unsqueeze(1).broadcast_to([128, 32, 64]),
                            in1=kmask[:].rearrange("p (b d) -> p b d", d=64), op=ALU.mult),
                           r=[('zbr', kt), 'kmask'], w=[('Kpad', kt)])
                    for kt in range(2):
                        op('pe', lambda e, kt=kt: e.matmul(BK[4][:, kt * 128:(kt + 1) * 128], lhsT=zb_r[kt][:, 128:256], rhs=ATs[kt][:],
                                                           start=(kt == 0), stop=False, skip_group_check=True),
                           r=[('ATs', kt), ('zbr', kt)], w=['roT'] if kt == 0 else [], sig=False)
                    for b in range(64):
                        op('pe', lambda e, b=b: e.matmul(BK[4][:, b * 4:(b + 1) * 4], lhsT=st_bfs[:, b * 128:(b + 1) * 128],
                                                         rhs=RTs[b // 32][0:64, (b % 32) * 4:(b % 32) * 4 + 4],
                                                         start=False, stop=(b == 63), skip_group_check=True),
                           r=['st_bfs'], w=['roT'] if b == 63 else [], sig=(b == 63))
                    op('act', lambda e: e.activation(out=roT[:], in_=BK[4][:, 0:256], func=AF.Copy), r=['roT'], w=['roT_sb'])
                    for tt in range(2):
                        op('pe', lambda e, tt=tt: e.transpose(BK[3][:, 256 + tt * 128:384 + tt * 128], roT[:, tt * 128:(tt + 1) * 128],
                                                              identf[:]), r=['roT_sb'], w=[('roTt', tt)])
                    for tt in range(2):
                        ret_epilogue(BK[3][:, 256 + tt * 128:384 + tt * 128], ('roTt', tt), (64 + tt) % 8, tt)
                    scat = []
                    for tt in range(2):
                        scat.append(op('sp', lambda e, tt=tt: e.dma_start(out=cat_s[tt * 128:(tt + 1) * 128, :], in_=cat_st[tt][:]),
                                       r=[('catd', tt), ('catr', tt)], dma=f'cat{tt}'))
                    for b in range(64):
                        bank = BK[5 + (b // 4) % 2]
                        op('pe', lambda e, b=b, bank=bank: e.matmul(bank[0:64, (b % 4) * 128:(b % 4 + 1) * 128],
                                                                    lhsT=Kpad[b // 32][:, (b % 32) * 64:(b % 32 + 1) * 64],
                                                                    rhs=zb_r[b // 32][:, 128:256], start=True, stop=True),
                           r=[('Kpad', b // 32)], w=[('DS', (b // 4) % 2)] if b % 4 == 0 else [], sig=(b % 4 == 3))
                        if b % 4 == 3:
                            b0 = b - 3
                            op('dve', lambda e, b0=b0, bank=bank: e.scalar_tensor_tensor(
                                out=st_sb[:, b0 * 128:(b0 + 4) * 128], in0=bank[0:64, :], scalar=dsc[0:64, 5:6],
                                in1=st_sb[:, b0 * 128:(b0 + 4) * 128], op0=ALU.mult, op1=ALU.add),
                               r=[('DS', (b // 4) % 2), 'st_sb', 'dsc'], w=[('stn', b0)])
                    op('sp', lambda e: e.dma_start(out=D['rs'].rearrange("b d e -> d b e"),
                                                   in_=st_sb[:].rearrange("d (b e) -> d b e", e=128)),
                       r=[('stn', b0) for b0 in range(0, 64, 4)], dma='rs')

                run_block(lambda: stage_a(1))
                es1.close()
                alloc_sample()
                run_block(lambda: stage_a(2))

        if stage == 'B':
            with ExitStack() as es:
                def sb(name, shape, dt):
                    return es.enter_context(nc.sbuf_tensor('s_' + name, shape, dt))

                identb = sb('identb2', [128, 128], BF16)
                mhalf = sb('mhalf2', [128, 1], F32)
                gffn = sb('gffn', [128, 1024], F32)
                gfin = sb('gfin', [128, 1024], F32)
                idx2 = sb('idx2', [128, 68], I32)
                w_out = sb('w_out', [128, 8 * 1024], BF16)
                wfo = sb('wfo', [128, 22 * 1024], BF16)
                wfi = [sb(f'wfi{i}', [128, 2048], BF16) for i in range(3)]
                wst32 = [sb(f'wst32_{i}', [128, 2048], F32) for i in range(2)]
                cat_sb = [sb(f'catsb{i}', [128, 1024], BF16) for i in range(2)]
                catT = [sb(f'catT{i}', [128, 1024], BF16) for i in range(2)]
                xres = [sb(f'xres{i}', [128, 1024], F32) for i in range(2)]
                xp1 = sb('xp1', [128, 5 * 1024], F32)
                junk = sb('junk2', [128, 1024], BF16)
                stat = sb('stat2', [128, 16], F32)
                h2 = [sb(f'h2{i}', [128, 1024], BF16) for i in range(2)]
                h2T = sb('h2T', [128, 8 * 640], BF16)
                actT = sb('actT', [128, 22 * 640], BF16)
                sgt = [sb(f'sgt{i}', [128, 512], F32) for i in range(2)]
                y_sb = [sb(f'y{i}', [128, 1024], F32) for i in range(2)]

                def stage_b():
                    def ld(dst, src, key, q='sp'):
                        op(q, lambda e, dst=dst, src=src: e.dma_start(out=dst, in_=src), w=[key], dma='c2_' + str(key))
                    ld(identb[:], D['identb'][:, :], 'identb2')
                    ld(gffn[:], D['gffn'][0:1, :].broadcast_to([128, 1024]), 'gffn')
                    ld(gfin[:], D['gfin'][0:1, :].broadcast_to([128, 1024]), 'gfin')
                    op('dve', lambda e: e.memset(mhalf[:], -0.5), w=['mhalf2'])
                    lcnt = [0]

                    def load_cast(dst, src, ncols, key):
                        sl_ = lcnt[0] % 2
                        lcnt[0] += 1
                        op('sp', lambda e: e.dma_start(out=wst32[sl_][:, 0:ncols], in_=src), w=[('wst32', sl_)], dma=f'wst{sl_}')
                        op('pool', lambda e: e.tensor_copy(out=dst, in_=wst32[sl_][:, 0:ncols]), r=[('wst32', sl_)], w=[key])
                    for kc in range(8):
                        load_cast(w_out[:, kc * 1024:(kc + 1) * 1024], D['w_out'][:, kc * 1024:(kc + 1) * 1024], 1024, ('wout', kc))
                    for j in range(22):
                        load_cast(wfo[:, j * 1024:(j + 1) * 1024], D['wfo'][j, :, :], 1024, ('wfo', j))

                    def rstd_chain(ssap, vap, outap, n, eps, keyin, keyout):
                        op('dve', lambda e: e.tensor_scalar(out=vap, in0=ssap, scalar1=1.0 / n, scalar2=eps,
                                                            op0=ALU.mult, op1=ALU.add), r=[keyin], w=[keyout + '_v'])
                        op('act', lambda e: e.activation(out=outap, in_=vap, func=AF.Sqrt), r=[keyout + '_v'], w=[keyout])
                        op('dve', lambda e: e.reciprocal(out=outap, in_=outap), r=[keyout], w=[keyout])
                        for _ in range(1):
                            op('dve', lambda e: e.scalar_tensor_tensor(out=ssap, in0=outap, scalar=vap, in1=outap,
                                                                       op0=ALU.mult, op1=ALU.mult), r=[keyout, keyout + '_v'], w=[keyin])
                            op('dve', lambda e: e.tensor_scalar(out=ssap, in0=ssap, scalar1=-0.5, scalar2=1.5,
                                                                op0=ALU.mult, op1=ALU.add), r=[keyin], w=[keyin])
                            op('dve', lambda e: e.tensor_tensor(out=outap, in0=outap, in1=ssap, op=ALU.mult),
                               r=[keyin, keyout], w=[keyout])

                    wcnt = [0]
                    for grp in GROUPS:
                        ng = len(grp)
                        NG = ng * 128
                        nch = [(0, min(512, NG))] + ([(512, NG - 512)] if NG > 512 else [])
                        for li, tt in enumerate(grp):
                            sl = tt % 2
                            op('sp', lambda e, tt=tt, sl=sl: e.dma_start(out=cat_sb[sl][:], in_=D['catin'][tt * 128:(tt + 1) * 128, :]),
                               w=[('catsb', sl)], dma=f'catsb{sl}')
                            op('sp', lambda e, tt=tt, sl=sl: e.dma_start(out=xres[sl][:], in_=D['xres'][tt * 128:(tt + 1) * 128, :]),
                               w=[('xres', sl)], dma=f'xres{sl}')
                            for half in range(2):
                                for k in range(4):
                                    ci = half * 4 + k
                                    op('pe', lambda e, k=k, ci=ci, sl=sl: e.matmul(BK[0][:, k * 128:(k + 1) * 128],
                                                                                   lhsT=cat_sb[sl][:, ci * 128:(ci + 1) * 128],
                                                                                   rhs=identb[:], start=True, stop=True),
                                       r=[('catsb', sl), 'identb2'], w=['bk0'] if k == 0 else [], sig=(k == 3))
                                if half == 0:
                                    op('act', lambda e, sl=sl: e.activation(out=catT[sl][:, 0:512], in_=BK[0][:, :], func=AF.Copy),
                                       r=['bk0'], w=[('catT', sl, 0)])
                                else:
                                    op('dve', lambda e, sl=sl: e.tensor_copy(out=catT[sl][:, 512:1024], in_=BK[0][:, :]),
                                       r=['bk0'], w=[('catT', sl, 1)])
                            for nh in range(2):
                                for ci in range(8):
                                    op('pe', lambda e, ci=ci, nh=nh, sl=sl: e.matmul(
                                        BK[1 + nh][:, :], lhsT=catT[sl][:, ci * 128:(ci + 1) * 128],
                                        rhs=w_out[:, ci * 1024 + nh * 512:ci * 1024 + (nh + 1) * 512], start=(ci == 0), stop=(ci == 7)),
                                       r=[('catT', sl, 0), ('catT', sl, 1), ('wout', ci)], w=[('bkx', nh)] if ci == 0 else [],
                                       sig=(ci == 7))
                            xv = xp1[:, li * 1024:(li + 1) * 1024]
                            for nh in range(2):
                                op('dve', lambda e, nh=nh, sl=sl, xv=xv: e.tensor_tensor(out=xv[:, nh * 512:(nh + 1) * 512],
                                                                                        in0=BK[1 + nh][:, :],
                                                                                        in1=xres[sl][:, nh * 512:(nh + 1) * 512],
                                                                                        op=ALU.add),
                                   r=[('bkx', nh), ('xres', sl)], w=[('xp1', li, nh)])
                            op('dve', lambda e, sl=sl, xv=xv: e.tensor_tensor(out=junk[:], in0=xv, in1=xv, op=ALU.mult), r=[('xp1', li, 0), ('xp1', li, 1)], w=['junk2'])
                            op('dve', lambda e, sl=sl, xv=xv: e.tensor_reduce(out=stat[:, sl:sl + 1], in_=junk[:], axis=AX.X, op=ALU.add), r=['junk2'], w=[('ss2', sl)])
                            rstd_chain(stat[:, sl:sl + 1], stat[:, 2 + sl:3 + sl], stat[:, 4 + sl:5 + sl], 1024.0, 1e-6,
                                       ('ss2', sl), f'rstd2{sl}')
                            op('dve', lambda e, sl=sl, xv=xv: e.scalar_tensor_tensor(out=h2[sl][:], in0=xv, scalar=stat[:, 4 + sl:5 + sl],
                                                                                     in1=gffn[:], op0=ALU.mult, op1=ALU.mult),
                               r=[('xp1', li, 0), ('xp1', li, 1), f'rstd2{sl}', 'gffn'], w=[('h2', sl)])
                            h2v = h2T[:].rearrange("p (k n) -> p k n", n=640)
                            for half in range(2):
                                for k in range(4):
                                    kc = half * 4 + k
                                    op('pe', lambda e, k=k, kc=kc, sl=sl: e.matmul(BK[0][:, k * 128:(k + 1) * 128],
                                                                                   lhsT=h2[sl][:, kc * 128:(kc + 1) * 128],
                                                                                   rhs=identb[:], start=True, stop=True),
                                       r=[('h2', sl)], w=['bk0'] if k == 0 else [], sig=(k == 3))
                                dst = h2v[:, half * 4:(half + 1) * 4, li * 128:(li + 1) * 128]
                                srcv = BK[0][:, :].rearrange("p (k n) -> p k n", n=128)
                                if half == 0:
                                    op('act', lambda e, dst=dst, srcv=srcv: e.activation(out=dst, in_=srcv, func=AF.Copy),
                                       r=['bk0'], w=[('h2T', li, 0)])
                                else:
                                    op('dve', lambda e, dst=dst, srcv=srcv: e.tensor_copy(out=dst, in_=srcv),
                                       r=['bk0'], w=[('h2T', li, 1)])
                        h2keys = [('h2T', li, hf) for li in range(ng) for hf in range(2)]
                        av = actT[:].rearrange("p (j n) -> p j n", n=640)
                        pc = 0
                        for j in range(22):
                            ws = wcnt[0] % 3
                            wcnt[0] += 1
                            load_cast(wfi[ws][:], D['wfi'][j, :, :], 2048, ('wfi', ws))
                            for (n0, nn) in nch:
                                pb = pc % 2
                                pc += 1
                                G, U = BK[3 + 2 * pb], BK[4 + 2 * pb]
                                for kc in range(8):
                                    op('pe', lambda e, kc=kc, ws=ws, n0=n0, nn=nn, G=G: e.matmul(
                                        G[:, 0:nn], lhsT=wfi[ws][:, kc * 256:kc * 256 + 128], rhs=h2v[:, kc, n0:n0 + nn],
                                        start=(kc == 0), stop=(kc == 7)),
                                       r=[('wfi', ws)] + h2keys, w=[('G', pb)] if kc == 0 else [], sig=(kc == 7))
                                for kc in range(8):
                                    op('pe', lambda e, kc=kc, ws=ws, n0=n0, nn=nn, U=U: e.matmul(
                                        U[:, 0:nn], lhsT=wfi[ws][:, kc * 256 + 128:kc * 256 + 256], rhs=h2v[:, kc, n0:n0 + nn],
                                        start=(kc == 0), stop=(kc == 7)),
                                       r=[('wfi', ws)], w=[('U', pb)] if kc == 0 else [], sig=(kc == 7))
                                op('act', lambda e, pb=pb, nn=nn, G=G: e.activation(out=sgt[pb][:, 0:nn], in_=G[:, 0:nn], func=AF.Silu),
                                   r=[('G', pb)], w=[('sgt', pb)])
                                op('dve', lambda e, pb=pb, nn=nn, n0=n0, j=j, U=U: e.tensor_tensor(
                                    out=av[:, j, n0:n0 + nn], in0=sgt[pb][:, 0:nn], in1=U[:, 0:nn], op=ALU.mult),
                                   r=[('sgt', pb), ('U', pb)], w=[('actT', j, n0)])
                        akeys = [('actT', j, n0) for j in range(22) for (n0, nn) in nch]
                        for li, tt in enumerate(grp):
                            ysl = tt % 2
                            xv = xp1[:, li * 1024:(li + 1) * 1024]
                            for nh in range(2):
                                for j in range(22):
                                    op('pe', lambda e, j=j, nh=nh, li=li: e.matmul(
                                        BK[1 + nh][:, :], lhsT=av[:, j, li * 128:(li + 1) * 128],
                                        rhs=wfo[:, j * 1024 + nh * 512:j * 1024 + (nh + 1) * 512], start=(j == 0), stop=(j == 21)),
                                       r=(akeys if j == 0 else []) + [('wfo', j)], w=[('bkx', nh)] if j == 0 else [], sig=(j == 21))
                                op('dve', lambda e, nh=nh, xv=xv: e.tensor_tensor(out=xv[:, nh * 512:(nh + 1) * 512],
                                                                                 in0=BK[1 + nh][:, :],
                                                                                 in1=xv[:, nh * 512:(nh + 1) * 512], op=ALU.add),
                                   r=[('bkx', nh)], w=[('xp1', li, nh)])
                            op('dve', lambda e, ysl=ysl, xv=xv: e.tensor_tensor(out=junk[:], in0=xv, in1=xv, op=ALU.mult), r=[('xp1', li, 0), ('xp1', li, 1)], w=['junk2'])
                            op('dve', lambda e, ysl=ysl, xv=xv: e.tensor_reduce(out=stat[:, 8 + ysl:9 + ysl], in_=junk[:], axis=AX.X, op=ALU.add), r=['junk2'], w=[('ss3', ysl)])
                            rstd_chain(stat[:, 8 + ysl:9 + ysl], stat[:, 10 + ysl:11 + ysl], stat[:, 12 + ysl:13 + ysl], 1024.0, 1e-6,
                                       ('ss3', ysl), f'rstd3{ysl}')
                            op('dve', lambda e, ysl=ysl, xv=xv: e.scalar_tensor_tensor(out=y_sb[ysl][:], in0=xv,
                                                                                       scalar=stat[:, 12 + ysl:13 + ysl], in1=gfin[:],
                                                                                       op0=ALU.mult, op1=ALU.mult),
                               r=[('xp1', li, 0), ('xp1', li, 1), f'rstd3{ysl}', 'gfin'], w=[('y', ysl)])
                            op('sp', lambda e, ysl=ysl, tt=tt: e.dma_start(out=D['y'][tt * 128:(tt + 1) * 128, :], in_=y_sb[ysl][:]),
                               r=[('y', ysl)], dma=f'yo{ysl}')

                run_block(stage_b)
        nc.all_engine_barrier()
        for sh in sem_pool:
            nc.gpsimd.sem_clear(sh)
        nc.all_engine_barrier()
    return nc


_CACHE = {}


def _consts(h):
    key = ('c', h)
    if key in _CACHE:
        return _CACHE[key]
    p = np.arange(128)
    pos = np.zeros((128, 66), np.float32)
    for t in range(64):
        pos[:, t] = t * 128 + p
    for t in (64, 65):
        pos[:, t] = 2048 + (p % 4)

    def table(dim, theta):
        inv = (1.0 / (np.float32(theta) ** (np.arange(0, dim, 2, dtype=np.float32) / np.float32(dim)))).astype(np.float32)
        ang = (pos[:, :, None] * inv[None, None, :]).astype(np.float32)
        return np.cos(ang).astype(np.float32), np.sin(ang).astype(np.float32)
    cd, sd = table(16, 500000.0)
    cr, sr = table(64, 10000.0)
    gamma = 1.0 - 2.0 ** (-5.0 - h)
    dsc = np.zeros((128, 8), np.float32)
    dsc[:, 0] = gamma ** (p + 1.0)
    dsc[:, 1] = gamma ** (-(p + 1.0)) / 8.0
    dsc[:, 2] = gamma ** ((p % 4) + 1.0)
    dsc[:, 3] = gamma ** (-((p % 4) + 1.0)) / 8.0
    dsc[:, 4] = gamma ** 128.0
    dsc[:, 5] = gamma ** 4.0
    maskT = (p[:, None] <= p[None, :]).astype(np.float32)
    col = np.arange(256)
    smask = ((p[:, None] // 4 == col[None, :] // 8) & (p[:, None] % 4 <= col[None, :] % 4)).astype(np.float32)
    rmask = ((p[:, None] // 4 == p[None, :] // 4) & (p[:, None] % 4 <= p[None, :] % 4)).astype(np.float32)
    kb = np.arange(32)
    kmask = np.repeat((p[:, None] // 4 == kb[None, :]).astype(np.float32), 64, axis=1)
    out = dict(
        identb=np.eye(128, dtype=np.float32).astype(ml_dtypes.bfloat16),
        identf=np.eye(128, dtype=np.float32),
        maskT=maskT.astype(ml_dtypes.bfloat16),
        cosd=np.ascontiguousarray(cd.reshape(128, 66 * 8)), sind=np.ascontiguousarray(sd.reshape(128, 66 * 8)),
        cosr=np.ascontiguousarray(cr.reshape(128, 66 * 32)), sinr=np.ascontiguousarray(sr.reshape(128, 66 * 32)),
        dsc=dsc, iota=p.astype(np.float32).reshape(128, 1), iota8=(p % 8).astype(np.float32).reshape(128, 1), smask=smask, rmask=rmask,
        kmask=kmask.astype(ml_dtypes.bfloat16),
    )
    _CACHE[key] = out
    return out


def kernel(x_prompt, x_sample, cache_diff_k, cache_diff_v, state_ret, page_table,
           norm_mix_g, w_in, diff_lambda, diff_subln_g, w_out, norm_ffn_g,
           w_ffn_in, w_ffn_out, norm_final_g):
    f = lambda a: np.ascontiguousarray(np.asarray(a, dtype=np.float32))
    x_prompt, x_sample = f(x_prompt), f(x_sample)
    w_in0, w_out0, wfi0, wfo0 = f(w_in)[0], f(w_out)[0], f(w_ffn_in)[0], f(w_ffn_out)[0]
    ck, cv = np.asarray(cache_diff_k)[0][:NPH], np.asarray(cache_diff_v)[0][:NPH]
    st = f(state_ret)[0]
    pt = np.asarray(page_table).astype(np.int32)
    if NPH != 2560:
        pt = pt % NPH
    for stg in ('A', 'B'):
        if ('nc', stg) not in _CACHE:
            _CACHE[('nc', stg)] = build_program(stg)
    rows = np.concatenate([np.arange(hp * 128, hp * 128 + 128) if half == 0 else np.arange(512 + hp * 128, 512 + hp * 128 + 128)
                           for hp in range(4) for half in range(2)])
    w_out_l = np.ascontiguousarray(w_out0[rows].reshape(8, 128, 1024).transpose(1, 0, 2).reshape(128, 8192))
    wg = wfi0[:, :2816].reshape(8, 128, 22, 128)
    wu = wfi0[:, 2816:].reshape(8, 128, 22, 128)
    wfi_l = np.ascontiguousarray(np.concatenate([wg, wu], axis=3).transpose(2, 1, 0, 3).reshape(22, 128, 2048))
    wfo_l = np.ascontiguousarray(wfo0.reshape(22, 128, 1024))
    xs_flat = x_sample.reshape(512, 1024)
    p = np.arange(128)
    in_maps = []
    in_maps_b = []
    for c in range(8):
        g, h = c // 4, c % 4
        cols = np.concatenate([np.arange(h * 128, h * 128 + 128), 512 + np.arange(h * 128, h * 128 + 128),
                               1024 + np.arange(h * 128, h * 128 + 128), 1536 + np.arange(h * 64, h * 64 + 64),
                               1792 + np.arange(h * 64, h * 64 + 64), 2048 + np.arange(h * 128, h * 128 + 128),
                               2560 + np.arange(h * 128, h * 128 + 128)])
        w_in_l = np.ascontiguousarray(w_in0[:, cols].reshape(8, 128, 768).transpose(1, 0, 2).reshape(128, 8 * 768))
        srow = np.concatenate([(np.arange(8 * c, 8 * c + 8)[:, None] * 4 + np.arange(4)[None, :]).reshape(-1),
                               ((64 + np.arange(8 * c, 8 * c + 8))[:, None] * 4 + np.arange(4)[None, :]).reshape(-1)])
        xres = np.concatenate([x_prompt[0, 1024 * c:1024 * (c + 1)], x_prompt[1, 1024 * c:1024 * (c + 1)],
                               xs_flat[srow], xs_flat[srow]], axis=0)
        cst = _consts(h)
        mA = dict(
            xp=x_prompt[g], xs=np.ascontiguousarray(xs_flat[256 * g:256 * (g + 1)]), w_in=w_in_l,
            gmix=f(norm_mix_g).reshape(1, 1024), gsl=f(diff_subln_g).reshape(1, 128), lamp=f(diff_lambda).reshape(1, 256),
            ckv=np.ascontiguousarray(np.concatenate([ck[:, :, h].reshape(NPH * 128, 128), cv[:, :, h].reshape(NPH * 128, 128)], axis=1), dtype=np.float32).reshape(NPH * 8, 4096),
            pt=np.ascontiguousarray(np.repeat(pt[64 * g:64 * (g + 1)].T, 8, axis=0)),
            st_ret=np.ascontiguousarray(st[64 * g:64 * (g + 1), h]),
        )
        mA.update(cst)
        mB = dict(xres=np.ascontiguousarray(xres), w_out=w_out_l, wfi=wfi_l, wfo=wfo_l,
                  gffn=f(norm_ffn_g).reshape(1, 1024), gfin=f(norm_final_g).reshape(1, 1024), identb=cst['identb'])
        in_maps.append(mA)
        in_maps_b.append(mB)
    resA = run_bass_kernel_spmd(_CACHE[('nc', 'A')], in_maps, core_ids=list(range(8)))
    RA = resA.results
    for c in range(8):
        catin = np.zeros((2176, 1024), ml_dtypes.bfloat16)
        for hp in range(4):
            catin[0:1024, hp * 256:(hp + 1) * 256] = np.asarray(RA[hp]['cat_p'])[1024 * c:1024 * (c + 1)]
            catin[1024:2048, hp * 256:(hp + 1) * 256] = np.asarray(RA[4 + hp]['cat_p'])[1024 * c:1024 * (c + 1)]
            catin[2048:2080, hp * 256:(hp + 1) * 256] = np.asarray(RA[hp]['cat_s'])[32 * c:32 * (c + 1)]
            catin[2080:2112, hp * 256:(hp + 1) * 256] = np.asarray(RA[4 + hp]['cat_s'])[32 * c:32 * (c + 1)]
        catin[2112:2176] = catin[2048:2112]
        in_maps_b[c]['catin'] = catin
    resB = run_bass_kernel_spmd(_CACHE[('nc', 'B')], in_maps_b, core_ids=list(range(8)))
    RB = resB.results
    y_prompt = np.zeros((2, 8192, 1024), np.float32)
    y_sample = np.zeros((512, 1024), np.float32)
    k_prompt = np.zeros((1, 2, 8192, 4, 2, 64), np.float32)
    v_prompt = np.zeros((1, 2, 8192, 4, 128), np.float32)
    ret_prompt = np.zeros((1, 2, 4, 64, 128), np.float32)
    k_sample = np.zeros((1, 128, 4, 4, 2, 64), np.float32)
    v_sample = np.zeros((1, 128, 4, 4, 128), np.float32)
    ret_sample = np.zeros((1, 128, 4, 64, 128), np.float32)
    for c in range(8):
        g, h = c // 4, c % 4
        y = np.asarray(RB[c]['y'])
        y_prompt[0, 1024 * c:1024 * (c + 1)] = y[0:1024]
        y_prompt[1, 1024 * c:1024 * (c + 1)] = y[1024:2048]
        srow = np.concatenate([(np.arange(8 * c, 8 * c + 8)[:, None] * 4 + np.arange(4)[None, :]).reshape(-1),
                               ((64 + np.arange(8 * c, 8 * c + 8))[:, None] * 4 + np.arange(4)[None, :]).reshape(-1)])
        y_sample[srow] = y[2048:2112]
        kv = np.asarray(RA[c]['kv'])
        k_prompt[0, g, :, h] = kv[0:8192, 0:128].reshape(8192, 2, 64)
        v_prompt[0, g, :, h] = kv[0:8192, 128:256]
        k_sample[0, 64 * g:64 * (g + 1), :, h] = kv[8192:8448, 0:128].reshape(64, 4, 2, 64)
        v_sample[0, 64 * g:64 * (g + 1), :, h] = kv[8192:8448, 128:256].reshape(64, 4, 128)
        ret_prompt[0, g, h] = np.asarray(RA[c]['rp'])
        ret_sample[0, 64 * g:64 * (g + 1), h] = np.asarray(RA[c]['rs'])
    return (y_prompt, y_sample.reshape(128, 4, 1024), k_prompt, v_prompt, ret_prompt, k_sample, v_sample, ret_sample)
```
